# Optimizing a Trainium2 kernel written in Bass

```python
import math
import jax, jax.numpy as jnp
from jax import lax
import numpy as np

D_MODEL = 2048
BATCH = 16
SEQ = 2048
DEPTH = 4

HEAD_DIM = 128
N_HEADS = D_MODEL // HEAD_DIM
DIFF_HEADS = N_HEADS // 4
FOX_HEADS = (N_HEADS - DIFF_HEADS) // 2
SB_HEADS = N_HEADS - DIFF_HEADS - FOX_HEADS
DIFF_QK_DIM = HEAD_DIM // 2
FOX_W = FOX_HEADS * HEAD_DIM
SB_W = SB_HEADS * HEAD_DIM
DIFF_W = DIFF_HEADS * HEAD_DIM
DIFF_QK_W = DIFF_HEADS * 2 * DIFF_QK_DIM
MIX_W = FOX_W + SB_W + DIFF_W
SPLIT_SIZES = (FOX_W, FOX_W, FOX_W, FOX_W,
               SB_W, SB_W, SB_W, SB_W,
               DIFF_QK_W, DIFF_QK_W, DIFF_W, DIFF_W,
               FOX_HEADS)
VALUE_COLS = (False, False, True, False,
              False, False, True, False,
              False, False, True, False,
              False)
IN_W = sum(SPLIT_SIZES)
Q_BLOCK = 128
DEEPNORM_ALPHA = (2 * DEPTH) ** 0.25
DEEPNORM_BETA = (8 * DEPTH) ** -0.25
LN_EPS = 1e-5
SUBLN_EPS = 1e-5
NEG_INF = -1e30

kernel_name = "hybrid_fox_stickbreak_diffattn_deepnorm"


def _split_points():
    pts, acc = [], 0
    for n in SPLIT_SIZES[:-1]:
        acc += n
        pts.append(acc)
    return pts


def _to_blocks(a):
    b, s = a.shape[:2]
    a = a.reshape((b, s // Q_BLOCK, Q_BLOCK) + a.shape[2:])
    return jnp.moveaxis(a, 1, 0)


def _from_blocks(a):
    a = jnp.moveaxis(a, 0, 1)
    return a.reshape((a.shape[0], a.shape[1] * a.shape[2]) + a.shape[3:])


def _layernorm(x, g, b):
    xf = x.astype(jnp.float32)
    mu = jnp.mean(xf, axis=-1, keepdims=True)
    var = jnp.mean(jnp.square(xf - mu), axis=-1, keepdims=True)
    y = (xf - mu) * lax.rsqrt(var + LN_EPS) * g.astype(jnp.float32) + b.astype(jnp.float32)
    return y.astype(x.dtype)


def forgetting_attention(q, k, v, log_f):
    s = q.shape[1]
    pos = jnp.arange(s)
    c = jnp.cumsum(log_f, axis=1)
    ck = jnp.moveaxis(c, 1, 2)
    scale = HEAD_DIM ** -0.5

    def block(args):
        qb, cq, tq = args
        logits = jnp.einsum('bqhd,bkhd->bhqk', qb, k).astype(jnp.float32) * scale
        logits = logits + jnp.moveaxis(cq, 1, 2)[..., None] - ck[:, :, None, :]
        causal = tq[:, None] >= pos[None, :]
        p = jax.nn.softmax(jnp.where(causal, logits, NEG_INF), axis=-1)
        return jnp.einsum('bhqk,bkhd->bqhd', p.astype(v.dtype), v)

    out = lax.map(block, (_to_blocks(q), _to_blocks(c), pos.reshape(-1, Q_BLOCK)))
    return _from_blocks(out)


def stick_breaking_attention(q, k, v):
    s = q.shape[1]
    pos = jnp.arange(s)
    scale = HEAD_DIM ** -0.5

    def block(args):
        qb, tq = args
        z = jnp.einsum('bqhd,bkhd->bhqk', qb, k).astype(jnp.float32) * scale
        strict = tq[:, None] > pos[None, :]
        log_beta = jax.nn.log_sigmoid(z)
        log_keep = jnp.where(strict, jax.nn.log_sigmoid(-z), 0.0)
        later = lax.cumsum(log_keep, axis=3, reverse=True) - log_keep
        w = jnp.where(strict, jnp.exp(log_beta + later), 0.0)
        return jnp.einsum('bhqk,bkhd->bqhd', w.astype(v.dtype), v)

    out = lax.map(block, (_to_blocks(q), pos.reshape(-1, Q_BLOCK)))
    return _from_blocks(out)


def differential_attention(q, k, v, lam, slopes):
    s = q.shape[1]
    pos = jnp.arange(s)
    scale = DIFF_QK_DIM ** -0.5

    def block(args):
        qb, tq = args
        logits = jnp.einsum('bqhcd,bkhcd->bhcqk', qb, k).astype(jnp.float32) * scale
        dist = (tq[:, None] - pos[None, :]).astype(jnp.float32)
        logits = logits - slopes[None, :, None, None, None] * dist
        causal = tq[:, None] >= pos[None, :]
        p = jax.nn.softmax(jnp.where(causal, logits, NEG_INF), axis=-1)
        attn = p[:, :, 0] - lam * p[:, :, 1]
        return jnp.einsum('bhqk,bkhd->bqhd', attn.astype(v.dtype), v)

    out = lax.map(block, (_to_blocks(q), pos.reshape(-1, Q_BLOCK)))
    return _from_blocks(out)


def hybrid_layer(x, w_in, b_f, lam_p, subln_g, w_out, ln_g, ln_b, layer_idx):
    b, s, _ = x.shape
    h = x @ w_in
    (fq, fk, fv, fg, sq, sk, sv, sg, dq, dk, dv, dg, ff) = jnp.split(h, _split_points(), axis=-1)

    log_f = jax.nn.log_sigmoid(ff.astype(jnp.float32) + b_f.astype(jnp.float32))
    hs = (b, s, FOX_HEADS, HEAD_DIM)
    o_fox = forgetting_attention(fq.reshape(hs), fk.reshape(hs), fv.reshape(hs), log_f)
    o_fox = o_fox.reshape(b, s, FOX_W) * jax.nn.silu(fg)

    hs = (b, s, SB_HEADS, HEAD_DIM)
    o_sb = stick_breaking_attention(sq.reshape(hs), sk.reshape(hs), sv.reshape(hs))
    o_sb = o_sb.reshape(b, s, SB_W) * jax.nn.silu(sg)

    lam_init = 0.8 - 0.6 * math.exp(-0.3 * layer_idx)
    lp = lam_p.astype(jnp.float32)
    lam = jnp.exp(jnp.sum(lp[0] * lp[1])) - jnp.exp(jnp.sum(lp[2] * lp[3])) + lam_init
    slopes = jnp.exp2(-8.0 * jnp.arange(1, DIFF_HEADS + 1, dtype=jnp.float32) / DIFF_HEADS)
    qs = (b, s, DIFF_HEADS, 2, DIFF_QK_DIM)
    o_d = differential_attention(dq.reshape(qs), dk.reshape(qs),
                                 dv.reshape(b, s, DIFF_HEADS, HEAD_DIM), lam, slopes)
    of = o_d.astype(jnp.float32)
    of = of * lax.rsqrt(jnp.mean(jnp.square(of), axis=-1, keepdims=True) + SUBLN_EPS)
    of = of * subln_g.astype(jnp.float32) * (1.0 - lam_init)
    o_d = of.astype(x.dtype).reshape(b, s, DIFF_W) * jax.nn.silu(dg)

    y = jnp.concatenate([o_fox, o_sb, o_d], axis=-1) @ w_out
    return _layernorm(DEEPNORM_ALPHA * x + y, ln_g, ln_b)


def setup_inputs(seed: int = 0) -> dict:
    key = jax.random.key(seed)
    ks = jax.random.split(key, 8)
    x = jax.random.normal(ks[0], (BATCH, SEQ, D_MODEL), jnp.float32)
    col_scale = jnp.concatenate([
        jnp.full((n,), DEEPNORM_BETA if is_v else 1.0, jnp.float32)
        for n, is_v in zip(SPLIT_SIZES, VALUE_COLS)])
    w_in = jax.random.normal(ks[1], (DEPTH, D_MODEL, IN_W), jnp.float32) * (D_MODEL ** -0.5) * col_scale
    b_f = jax.random.uniform(ks[2], (DEPTH, FOX_HEADS), jnp.float32, 1.0, 5.0)
    diff_lambda = 0.1 * jax.random.normal(ks[3], (DEPTH, 4, DIFF_QK_DIM), jnp.float32)
    diff_subln_g = 1.0 + 0.02 * jax.random.normal(ks[4], (DEPTH, HEAD_DIM), jnp.float32)
    w_out = jax.random.normal(ks[5], (DEPTH, MIX_W, D_MODEL), jnp.float32) * (MIX_W ** -0.5) * DEEPNORM_BETA
    ln_g = 1.0 + 0.02 * jax.random.normal(ks[6], (DEPTH, D_MODEL), jnp.float32)
    ln_b = 0.02 * jax.random.normal(ks[7], (DEPTH, D_MODEL), jnp.float32)
    return {"x": x, "w_in": w_in, "b_f": b_f, "diff_lambda": diff_lambda,
            "diff_subln_g": diff_subln_g, "w_out": w_out, "ln_g": ln_g, "ln_b": ln_b}


def reference(x, w_in, b_f, diff_lambda, diff_subln_g, w_out, ln_g, ln_b):
    for l in range(DEPTH):
        x = hybrid_layer(x, w_in[l], b_f[l], diff_lambda[l], diff_subln_g[l],
                         w_out[l], ln_g[l], ln_b[l], l)
    return x
```

```python
import math
from contextlib import ExitStack

import numpy as np
import concourse.bass as bass
import concourse.mybir as mybir
from concourse.bass_utils import run_bass_kernel_spmd

F32 = mybir.dt.float32
BF16 = mybir.dt.bfloat16
AF = mybir.ActivationFunctionType
ALU = mybir.AluOpType
AX = mybir.AxisListType

FULL_CFG = dict(S=2048, NF=6, NS=6, ND=4, DEPTH=4, NSEQ=2)
DEEPNORM_ALPHA = (2 * 4) ** 0.25
LN_EPS = 1e-5
SUBLN_EPS = 1e-5
STREAMS = ("pe", "act", "dve", "pool", "sp")
MAXC = 20000
SAME_ENGINE_RAW = True


class Tile:
    __slots__ = ("name", "w", "r")

    def __init__(self, name):
        self.name = name
        self.w = None
        self.r = []


class Op:
    __slots__ = ("stream", "idx", "emit", "deps", "dma_sem", "dma_val", "signal", "signo", "raw_same")

    def __init__(self, stream, idx, emit):
        self.stream = stream
        self.idx = idx
        self.emit = emit
        self.deps = []
        self.dma_sem = None
        self.dma_val = 0
        self.signal = False
        self.signo = 0
        self.raw_same = False


class Sched:
    def __init__(self):
        self.streams = {s: [] for s in STREAMS}
        self.pending = {s: [] for s in STREAMS}
        self.dma_since_barrier = []
        self.dma_counts = {}

    def add(self, stream, emit, reads=(), writes=(), dma_sem=None, ndma=1):
        op = Op(stream, len(self.streams[stream]), emit)
        deps = {}

        def dep(o, raw):
            if o is None:
                return
            k = id(o)
            if k in deps:
                deps[k] = (o, deps[k][1] or raw)
            else:
                deps[k] = (o, raw)

        for t in reads:
            dep(t.w, True)
        for t in writes:
            dep(t.w, False)
            for o in t.r:
                dep(o, False)
        for o in self.pending[stream]:
            dep(o, True)
        self.pending[stream] = []
        best = {}
        out = []
        for o, raw in deps.values():
            if o.dma_sem is not None:
                out.append((o, raw))
            else:
                b = best.get(o.stream)
                if b is None or o.idx > b[0].idx:
                    best[o.stream] = (o, raw)
                elif o.idx == b[0].idx and raw:
                    best[o.stream] = (o, True)
        for s, (o, raw) in best.items():
            if s == stream and o.dma_sem is None:
                if stream == "pe" or stream == "sp":
                    continue
            out.append((o, raw))
        op.deps = [o for o, _ in out]
        for o in op.deps:
            o.signal = True
        if dma_sem is not None:
            op.dma_sem = dma_sem
            c = self.dma_counts.get(id(dma_sem), 0) + ndma
            self.dma_counts[id(dma_sem)] = c
            op.dma_val = 16 * c
            self.dma_since_barrier.append(op)
        for t in reads:
            t.r.append(op)
            if len(t.r) > 24:
                keep = {}
                dm = []
                for o in t.r:
                    if o.dma_sem is not None:
                        dm.append(o)
                    else:
                        k = keep.get(o.stream)
                        if k is None or o.idx > k.idx:
                            keep[o.stream] = o
                t.r = list(keep.values()) + dm[-16:]
                if len(dm) > 16:
                    t.r = list(keep.values()) + dm
        for t in writes:
            t.w = op
            t.r = []
        self.streams[stream].append(op)
        return op

    def barrier(self):
        lasts = [self.streams[s][-1] for s in STREAMS if self.streams[s]]
        for s in STREAMS:
            self.pending[s] = self.pending[s] + list(lasts) + list(self.dma_since_barrier)
        self.dma_since_barrier = []

    def finish(self):
        self.barrier()
        self.add("sp", None)

    def emit_all(self, nc, es):
        nsig = {}
        for s in STREAMS:
            n = 0
            for op in self.streams[s]:
                if op.signal and op.dma_sem is None:
                    n += 1
                    op.signo = n
            nsig[s] = n
        sems = {}
        for s in STREAMS:
            nb = max(1, (nsig[s] + MAXC - 1) // MAXC)
            sems[s] = [es.enter_context(nc.semaphore(f"sem_{s}_{i}")) for i in range(nb)]
        block = es.enter_context(nc.Block())

        def run(stream, eng):
            waited = {s: 0 for s in STREAMS}
            dwaited = {}
            for op in self.streams[stream]:
                for d in op.deps:
                    if d.dma_sem is not None:
                        k = id(d.dma_sem)
                        if dwaited.get(k, 0) >= d.dma_val:
                            continue
                        eng.wait_ge(d.dma_sem, d.dma_val)
                        dwaited[k] = d.dma_val
                    else:
                        if waited[d.stream] >= d.signo:
                            continue
                        n = d.signo - 1
                        eng.wait_ge(sems[d.stream][n // MAXC], (n % MAXC) + 1)
                        waited[d.stream] = d.signo
                if op.emit is None:
                    continue
                ins = op.emit(eng)
                if op.dma_sem is not None:
                    pass
                elif op.signal:
                    n = op.signo - 1
                    ins.then_inc(sems[stream][n // MAXC], 1)

        @block.tensor
        def _(e):
            run("pe", e)

        @block.scalar
        def _(e):
            run("act", e)

        @block.vector
        def _(e):
            run("dve", e)

        @block.gpsimd
        def _(e):
            run("pool", e)

        @block.sync
        def _(e):
            run("sp", e)


def build_nc(cfg):
    S, NF, NS, ND, DEPTH, NSEQ = (cfg[k] for k in ("S", "NF", "NS", "ND", "DEPTH", "NSEQ"))
    NH = NF + NS + ND
    D = 128 * NH
    KC = NH
    NT = S // 128
    NG = S // 512
    CBW = min(512, D)
    NCB = D // CBW
    assert NCB * CBW == D and NCB <= 4
    NWS = 5
    NFp = max(NF, 1)

    nc = bass.Bass("TRN2", target_bir_lowering=False)
    x_d = nc.dram_tensor("x", [NSEQ, S, D], F32, kind="ExternalInput").ap()
    win_d = nc.dram_tensor("w_in_r", [DEPTH, NH, 4, 128, KC, 128], F32, kind="ExternalInput").ap()
    wff_d = nc.dram_tensor("w_ff_r", [DEPTH, 128, KC, NFp], F32, kind="ExternalInput").ap()
    wout_d = nc.dram_tensor("w_out_r", [DEPTH, NCB, 128, KC, CBW], F32, kind="ExternalInput").ap()
    bf_d = nc.dram_tensor("b_f", [DEPTH, NFp], F32, kind="ExternalInput").ap()
    dl_d = nc.dram_tensor("diff_lambda", [DEPTH, 256], F32, kind="ExternalInput").ap()
    sg_d = nc.dram_tensor("diff_subln_g", [DEPTH, 128], F32, kind="ExternalInput").ap()
    lng_d = nc.dram_tensor("ln_g", [DEPTH, D], F32, kind="ExternalInput").ap()
    lnb_d = nc.dram_tensor("ln_b", [DEPTH, D], F32, kind="ExternalInput").ap()
    out_d = nc.dram_tensor("out", [NSEQ, S, D], F32, kind="ExternalOutput").ap()
    scr_d = nc.dram_tensor("scr", [2, S, D], F32, kind="Internal").ap()

    sch = Sched()
    es = ExitStack()
    E = es.enter_context

    def sb(name, shape, dt):
        return E(nc.sbuf_tensor(name, shape, dt))

    def dsem(name):
        return E(nc.semaphore(name))

    RSZ = max(KC * S, NCB * KC * CBW)
    R = sb("R", [128, RSZ], BF16)
    xT = R[:, 0:KC * S].rearrange("p (k s) -> p k s", k=KC)
    wo = [R[:, cb * KC * CBW:(cb + 1) * KC * CBW].rearrange("p (k c) -> p k c", k=KC) for cb in range(NCB)]
    OTSZ = max(NH * S, 2 * 2 * D + 2 * D)
    OTr = sb("OT", [128, OTSZ], BF16)
    OT = OTr[:, 0:NH * S].rearrange("p (h s) -> p h s", h=NH)
    NXB = 4
    xb = [OTr[:, i * D:(i + 1) * D] for i in range(NXB)]
    QKVSZ = max(2 * 3 * S, 3 * 2 * D)
    QKVr = sb("QKV", [128, QKVSZ], BF16)
    QT = [QKVr[:, (3 * s + 0) * S:(3 * s + 1) * S] for s in range(2)]
    KT = [QKVr[:, (3 * s + 1) * S:(3 * s + 2) * S] for s in range(2)]
    VV = [QKVr[:, (3 * s + 2) * S:(3 * s + 3) * S].rearrange("p (t d) -> p t d", d=128) for s in range(2)]
    zsl = [QKVr[:, i * 2 * D:(i + 1) * 2 * D].bitcast(F32) for i in range(3)]
    Wsl = [sb(f"W{i}", [128, KC, 128], BF16) for i in range(NWS)]
    wff = sb("wff", [128, KC, NFp], BF16)
    MISC_P2 = 3 * 512 + 2 * 512 + 2 * 512 + 2 * 512
    MISC_F32 = 2 * NT * NT + 6 * 512
    MSZ = max(MISC_P2 + 2 * MISC_F32, 2 * 2 * D)
    Mr = sb("MISC", [128, MSZ], BF16)
    off = 0

    def carve(n, dt=BF16):
        nonlocal off
        if dt == F32:
            a = Mr[:, off:off + 2 * n].bitcast(F32)
            off += 2 * n
        else:
            a = Mr[:, off:off + n]
            off += n
        return a

    PT = [carve(512) for _ in range(3)]
    SP_ = [carve(512) for _ in range(2)]
    ACCB = [carve(512) for _ in range(2)]
    ACC32 = carve(512, F32)
    BIASF = [carve(NT * NT, F32).rearrange("p (j i) -> p j i", j=NT) for _ in range(2)]
    TMP = [carve(512, F32) for _ in range(4)]
    GT = [carve(512, F32) for _ in range(2)]
    assert off <= MSZ
    lng_b = Mr[:, 0:2 * D].bitcast(F32)
    lnb_b = Mr[:, 2 * D:4 * D].bitcast(F32)

    identB = sb("identB", [128, 128], BF16)
    onesB = sb("onesB", [128, 128], BF16)
    zerosB = sb("zerosB", [128, 128], BF16)
    negOnesB = sb("negOnesB", [128, 128], BF16)
    negTriB = sb("negTriB", [128, 128], BF16)
    maskC = sb("maskC", [128, 128], BF16)
    maskS = sb("maskS", [128, 128], BF16)
    negF = TMP[0][:, 0:128]
    cF = TMP[1][:, 0:128]
    bigF = TMP[2][:, 0:128]
    onesF = sb("onesF", [128, 128], F32)
    triuF = sb("triuF", [128, 128], F32)
    sel64F = sb("sel64F", [128, 128], F32)
    meanF = sb("meanF", [128, 128], F32)
    tokF = sb("tokF", [128, NT], F32)
    NGI = 2 * NG
    biasD = sb("biasD", [128, max(ND, 1), NT, NGI], F32)
    bfb = sb("bfb", [128, NFp], F32)
    dlb = sb("dlb", [128, 256], F32)
    sgc = sb("sgc", [128, 1], F32)
    gcol = sb("gcol", [128, 1], F32)
    lamt = sb("lamt", [128, 8], F32)
    dtmp = sb("dtmp", [128, 64], F32)
    nl = sb("nl", [128, NT, NFp], F32)
    Tsb = sb("Tsb", [128, NT, NFp], F32)
    pre = sb("pre", [128, NT, NFp], F32)
    ncol = sb("ncol", [128, NT, NFp], F32)
    midsb = sb("midsb", [128, NT, NFp], F32)
    bnst = sb("bnst", [128, 8, 6], F32)
    mv = sb("mv", [128, 8], F32)

    pb = [E(nc.psum_tensor(f"pb{i}", [128, 512], F32)) for i in range(8)]

    T = Tile
    t_xT = [T(f"xT{t}") for t in range(NT)]
    t_wo = [T(f"wo{c}") for c in range(NCB)]
    t_OT = [[T(f"OT{b}_{g}") for g in range(NG)] for b in range(NH)]
    t_xb = [T(f"xb{i}") for i in range(NXB)]
    t_QT = [[T(f"QT{s}_{g}") for g in range(NG)] for s in range(2)]
    t_KT = [[T(f"KT{s}_{g}") for g in range(NG)] for s in range(2)]
    t_V = [[T(f"V{s}_{g}") for g in range(NG)] for s in range(2)]
    t_z = [T(f"z{i}") for i in range(3)]
    t_W = [T(f"W{i}") for i in range(NWS)]
    t_wff = T("wff")
    t_PT = [T(f"PT{i}") for i in range(3)]
    t_SP = [T(f"SP{i}") for i in range(2)]
    t_ACCB = [T(f"ACCB{i}") for i in range(2)]
    t_ACC32 = T("ACC32")
    t_BIASF = [T("BIASF0"), T("BIASF1")]
    t_TMP = [T(f"TMP{i}") for i in range(4)]
    t_GT = [T("GT0"), T("GT1")]
    t_lnp = T("lnp")
    t_const = T("const")
    t_par = T("par")
    t_lam = T("lam")
    t_fox = T("foxprep")
    t_st = T("stats")
    t_pb = [T(f"pb{i}") for i in range(8)]
    t_dram = T("dram")

    s_xb = [dsem(f"s_xb{i}") for i in range(NXB)]
    s_zin = [dsem(f"s_zin{i}") for i in range(3)]
    s_zout = [dsem(f"s_zout{i}") for i in range(3)]
    s_W = [dsem(f"s_W{i}") for i in range(NWS)]
    s_wff = dsem("s_wff")
    s_wo = [dsem(f"s_wo{i}") for i in range(NCB)]
    s_lnp = dsem("s_lnp")
    s_par = dsem("s_par")

    pe = lambda emit, r=(), w=(): sch.add("pe", emit, r, w)
    act = lambda emit, r=(), w=(): sch.add("act", emit, r, w)
    dve = lambda emit, r=(), w=(): sch.add("dve", emit, r, w)
    pool = lambda emit, r=(), w=(): sch.add("pool", emit, r, w)

    def dma(stream, sem, out, in_, r=(), w=()):
        def emit(e):
            return e.dma_start(out=out, in_=in_).then_inc(sem, 16)
        return sch.add(stream, emit, r, w, dma_sem=sem)

    def consts():
        pool(lambda e: e.memset(onesF[:], 1.0), (), [t_const])
        pool(lambda e: e.memset(negF, -1.0), (), [t_const])
        pool(lambda e: e.memset(meanF[:], 1.0 / 128.0), (), [t_const])
        pool(lambda e: e.memset(zerosB[:], 0.0), (), [t_const])
        pool(lambda e: e.memset(onesB[:], 1.0), (), [t_const])
        pool(lambda e: e.memset(negOnesB[:], -1.0), (), [t_const])

        def sel(out, in_, pat, cm, base, op):
            pool(lambda e: e.affine_select(out=out, in_=in_, pattern=pat, compare_op=op, fill=0.0,
                                           base=base, channel_multiplier=cm), [t_const], [t_const])

        sel(triuF[:], onesF[:], [[1, 128]], -1, 0, ALU.is_ge)
        sel(sel64F[:], onesF[:], [[0, 128]], 1, -64, ALU.is_equal)
        sel(cF, onesF[:], [[1, 128]], -1, 0, ALU.is_equal)
        pool(lambda e: e.tensor_copy(out=identB[:], in_=cF), [t_const], [t_const])
        pool(lambda e: e.memset(bigF, -30000.0), (), [t_const])
        sel(cF, bigF, [[-1, 128]], 1, -1, ALU.is_ge)
        pool(lambda e: e.tensor_copy(out=maskC[:], in_=cF), [t_const], [t_const])
        sel(cF, bigF, [[-1, 128]], 1, 0, ALU.is_ge)
        pool(lambda e: e.tensor_copy(out=maskS[:], in_=cF), [t_const], [t_const])
        sel(cF, negF, [[-1, 128]], 1, 0, ALU.is_ge)
        pool(lambda e: e.tensor_copy(out=negTriB[:], in_=cF), [t_const], [t_const])
        pool(lambda e: e.iota(tokF[:], pattern=[[128, NT]], base=0, channel_multiplier=1,
                              allow_small_or_imprecise_dtypes=True), (), [t_const])
        for d in range(ND):
            slope = 2.0 ** (-8.0 * (d + 1) / ND)
            for gi in range(NGI):
                cen = 256 * gi + 128
                pool(lambda e, d=d, gi=gi, cen=cen, slope=slope: e.tensor_scalar(
                    out=biasD[:, d, :, gi], in0=tokF[:], scalar1=float(-cen), scalar2=float(slope),
                    op0=ALU.add, op1=ALU.mult), [t_const], [t_const])

    evac_rr = [0]

    def evac_copy(out, in_, r, w):
        evac_rr[0] += 1
        if evac_rr[0] % 2:
            dve(lambda e: e.tensor_copy(out=out, in_=in_), r, w)
        else:
            act(lambda e: e.activation(out=out, in_=in_, func=AF.Copy), r, w)

    def phase1(src):
        for t in range(NT):
            sl = t % NXB
            dma("pool", s_xb[sl], xb[sl], src[t * 128:(t + 1) * 128, :], (), [t_xb[sl]])
            for c0 in range(0, KC, 8):
                n = min(8, KC - c0)
                bi = ((t * ((KC + 7) // 8)) + c0 // 8) % 2
                bank = pb[bi][:].bitcast(BF16)

                def tr(e, sl=sl, c0=c0, n=n, bank=bank):
                    ins = None
                    for i in range(n):
                        ins = e.transpose(bank[:, i * 128:(i + 1) * 128], xb[sl][:, (c0 + i) * 128:(c0 + i + 1) * 128], identB[:])
                    return ins
                pe(tr, [t_xb[sl], t_const], [t_pb[bi]])
                evac_copy(xT[:, c0:c0 + n, t * 128:(t + 1) * 128],
                          bank[:, 0:n * 128].rearrange("p (c s) -> p c s", c=n), [t_pb[bi]], [t_xT[t]])

    wlist = [(l, b, c) for _sq in range(NSEQ) for l in range(DEPTH) for b in range(NH) for c in (0, 1, 3, 2)]
    wissued = [0]
    wuse = [0]
    WAHEAD = 2

    def w_prefetch(upto):
        while wissued[0] <= min(upto, len(wlist) - 1):
            n = wissued[0]
            l, b, c = wlist[n]
            i = n % NWS
            dma("pool", s_W[i], Wsl[i][:], win_d[l, b, c], (), [t_W[i]])
            wissued[0] += 1

    def next_w(l, b, c):
        n = wuse[0]
        assert wlist[n] == (l, b, c)
        wuse[0] += 1
        w_prefetch(n + WAHEAD)
        return n % NWS

    ipb = [0]
    gctr = [0]

    def inproj_bank():
        ipb[0] += 1
        return ipb[0] % 2

    def inproj_head(l, b, slot, kind):
        qscale = (64 if kind == "d" else 128) ** -0.5
        wi = {}
        pend = []

        def flush(n):
            for _ in range(n):
                if pend:
                    pend.pop(0)()

        for c in (0, 1, 3):
            wi[c] = next_w(l, b, c)
            for tg in range(NG):
                flush(2)
                bi = inproj_bank()

                def mm(e, bi=bi, w=wi[c], tg=tg):
                    ins = None
                    for kc in range(KC):
                        ins = e.matmul(pb[bi][:], lhsT=Wsl[w][:, kc, :], rhs=xT[:, kc, tg * 512:(tg + 1) * 512],
                                       start=(kc == 0), stop=(kc == KC - 1))
                    return ins
                pe(mm, [t_W[wi[c]]] + t_xT[4 * tg:4 * tg + 4], [t_pb[bi]])
                cs = slice(tg * 512, (tg + 1) * 512)
                if c == 0:
                    dve(lambda e, bi=bi, cs=cs: e.tensor_scalar(out=QT[slot][:, cs], in0=pb[bi][:], scalar1=float(qscale),
                                                                scalar2=None, op0=ALU.mult), [t_pb[bi]], [t_QT[slot][tg]])
                elif c == 1:
                    dve(lambda e, bi=bi, cs=cs: e.tensor_copy(out=KT[slot][:, cs], in_=pb[bi][:]), [t_pb[bi]], [t_KT[slot][tg]])
                else:
                    gi_ = gctr[0] % 2
                    gctr[0] += 1
                    gt = GT[gi_]
                    tgt = t_GT[gi_]
                    ot_w = [t_OT[b][tg]] + (t_xb if b * S < NXB * D else [])
                    dve(lambda e, bi=bi, cs=cs: e.tensor_copy(out=OT[:, b, cs], in_=pb[bi][:]), [t_pb[bi]], ot_w)
                    act(lambda e, cs=cs, gt=gt: e.activation(out=gt, in_=OT[:, b, cs], func=AF.Exp, scale=-1.0), [t_OT[b][tg]], [tgt])
                    pend.append(lambda gt=gt, tgt=tgt: act(lambda e: e.activation(out=gt, in_=gt, func=AF.Ln, bias=1.0), [tgt], [tgt]))
                    pend.append(lambda gt=gt, tgt=tgt: act(lambda e: e.activation(out=gt, in_=gt, func=AF.Exp, scale=-1.0), [tgt], [tgt]))
                    pend.append(lambda gt=gt, tgt=tgt, cs=cs, tg=tg: pool(
                        lambda e: e.tensor_tensor(out=OT[:, b, cs], in0=OT[:, b, cs], in1=gt, op=ALU.mult), [tgt, t_OT[b][tg]], [t_OT[b][tg]]))
                yield
        wi[2] = next_w(l, b, 2)
        for tg in range(NG):
            flush(2)
            bi = inproj_bank()

            def mmv(e, bi=bi, w=wi[2], tg=tg):
                ins = None
                for u in range(4):
                    t = 4 * tg + u
                    for kc in range(KC):
                        ins = e.matmul(pb[bi][:, u * 128:(u + 1) * 128], lhsT=xT[:, kc, t * 128:(t + 1) * 128],
                                       rhs=Wsl[w][:, kc, :], start=(kc == 0), stop=(kc == KC - 1))
                return ins
            pe(mmv, [t_W[wi[2]]] + t_xT[4 * tg:4 * tg + 4], [t_pb[bi]])
            dve(lambda e, bi=bi, tg=tg: e.tensor_copy(out=VV[slot][:, 4 * tg:4 * tg + 4, :],
                                                      in_=pb[bi][:].rearrange("p (u d) -> p u d", u=4)),
                [t_pb[bi]], [t_V[slot][tg]])
            yield
        while pend:
            flush(2)
            yield

    def load_params(l):
        lam_init = 0.8 - 0.6 * math.exp(-0.3 * l)

        def emit(e):
            e.dma_start(out=bfb[:], in_=bf_d[l:l + 1, :].partition_broadcast(128)).then_inc(s_par, 16)
            e.dma_start(out=dlb[:], in_=dl_d[l:l + 1, :].partition_broadcast(128)).then_inc(s_par, 16)
            return e.dma_start(out=sgc[:], in_=sg_d[l:l + 1, :].rearrange("o p -> p o")).then_inc(s_par, 16)
        sch.add("sp", emit, (), [t_par], dma_sem=s_par, ndma=3)
        if ND > 0:
            dve(lambda e: e.tensor_tensor(out=dtmp[:], in0=dlb[:, 0:64], in1=dlb[:, 64:128], op=ALU.mult), [t_par], [t_lam])
            dve(lambda e: e.reduce_sum(out=lamt[:, 0:1], in_=dtmp[:], axis=AX.X), [t_lam], [t_lam])
            dve(lambda e: e.tensor_tensor(out=dtmp[:], in0=dlb[:, 128:192], in1=dlb[:, 192:256], op=ALU.mult), [t_par, t_lam], [t_lam])
            dve(lambda e: e.reduce_sum(out=lamt[:, 1:2], in_=dtmp[:], axis=AX.X), [t_lam], [t_lam])
            act(lambda e: e.activation(out=lamt[:, 2:4], in_=lamt[:, 0:2], func=AF.Exp), [t_lam], [t_lam])
            dve(lambda e: e.scalar_tensor_tensor(out=lamt[:, 4:5], in0=lamt[:, 3:4], scalar=float(-lam_init), in1=lamt[:, 2:3],
                                                 op0=ALU.add, op1=ALU.subtract), [t_lam], [t_lam])
            dve(lambda e: e.tensor_scalar(out=gcol[:], in0=sgc[:], scalar1=float(1.0 - lam_init), scalar2=None, op0=ALU.mult),
                [t_par, t_lam], [t_lam])

    def fox_prep(l):
        NC_ = NT * NF
        dma("pool", s_wff, wff[:], wff_d[l], (), [t_wff])

        def mmf(e):
            ins = None
            for t in range(NT):
                for kc in range(KC):
                    ins = e.matmul(pb[5][:, t * NF:(t + 1) * NF], lhsT=xT[:, kc, t * 128:(t + 1) * 128], rhs=wff[:, kc, 0:NF],
                                   start=(kc == 0), stop=(kc == KC - 1))
            return ins
        pe(mmf, [t_wff] + t_xT, [t_pb[5]])
        yield
        for t in range(NT):
            dve(lambda e, t=t: e.tensor_tensor(out=nl[:, t, 0:NF], in0=pb[5][:, t * NF:(t + 1) * NF], in1=bfb[:, 0:NF], op=ALU.add),
                [t_pb[5], t_par], [t_fox])
        nl2 = nl[:].rearrange("p t f -> p (t f)")
        act(lambda e: e.activation(out=nl2, in_=nl2, func=AF.Exp, scale=-1.0), [t_fox], [t_fox])
        act(lambda e: e.activation(out=nl2, in_=nl2, func=AF.Ln, bias=1.0), [t_fox], [t_fox])
        yield
        pe(lambda e: e.matmul(pb[5][:, 0:NC_], lhsT=onesF[:], rhs=nl2, start=True, stop=True), [t_fox, t_const], [t_pb[5]])
        pe(lambda e: e.matmul(pb[5][:, 128:128 + NC_], lhsT=triuF[:], rhs=nl2, start=True, stop=True), [t_fox, t_const], [t_pb[5]])
        dve(lambda e: e.tensor_copy(out=Tsb[:].rearrange("p t f -> p (t f)"), in_=pb[5][:, 0:NC_]), [t_pb[5]], [t_fox])
        yield
        dve(lambda e: e.memset(pre[:, 0, :], 0.0), [t_fox], [t_fox])
        for j in range(1, NT):
            dve(lambda e, j=j: e.tensor_tensor(out=pre[:, j, :], in0=pre[:, j - 1, :], in1=Tsb[:, j - 1, :], op=ALU.add), [t_fox], [t_fox])
        dve(lambda e: e.tensor_tensor(out=ncol[:].rearrange("p t f -> p (t f)"), in0=pb[5][:, 128:128 + NC_],
                                      in1=pre[:].rearrange("p t f -> p (t f)"), op=ALU.add), [t_pb[5], t_fox], [t_fox])
        pe(lambda e: e.matmul(pb[5][:, 256:256 + NC_], lhsT=sel64F[:], rhs=ncol[:].rearrange("p t f -> p (t f)"), start=True, stop=True),
           [t_fox, t_const], [t_pb[5]])
        dve(lambda e: e.tensor_copy(out=midsb[:].rearrange("p t f -> p (t f)"), in_=pb[5][:, 256:256 + NC_]), [t_pb[5]], [t_fox])

    def fox_bias(h, bs):
        def em(e):
            ins = None
            for i in range(NT):
                ins = e.tensor_scalar(out=BIASF[bs][:, :, i], in0=ncol[:, :, h], scalar1=midsb[:, i, h:h + 1], scalar2=None,
                                      op0=ALU.subtract)
            return ins
        dve(em, [t_fox], [t_BIASF[bs]])

    ptc = [0]
    sbk = [0]

    def blk_geom(g, j):
        r = j - 4 * g
        q0 = 128 * r if r >= 0 else 0
        return r, q0

    def softmax_attn(b, slot, g, maps, exp_emit, obanks, dbanks, extra_r):
        nblk = 4 * g + 4
        items = [(j, m) for j in range(nblk) for m in range(len(maps))]
        st = {}

        def stage_s(n):
            j, m = items[n]
            r, q0 = blk_geom(g, j)
            sbk[0] += 1
            bi = 2 + sbk[0] % 2
            ptc[0] += 1
            pi = ptc[0] % 3
            st[n] = (bi, pi)
            lh, rh = maps[m]
            def mms(e):
                ins = e.matmul(pb[bi][:, q0:512], lhsT=lh(j), rhs=rh(g, q0), start=True, stop=(r < 0))
                if r >= 0:
                    ins = e.matmul(pb[bi][:, q0:q0 + 128], lhsT=identB[:], rhs=maskC[:], start=False, stop=True)
                return ins
            pe(mms, [t_KT[slot][j // 4], t_QT[slot][g], t_const], [t_pb[bi]])
            exp_emit(m, j, g, r, q0, pb[bi], PT[pi], [t_pb[bi]] + extra_r, [t_PT[pi]])

        def stage_pv(n):
            j, m = items[n]
            r, q0 = blk_geom(g, j)
            bi, pi = st[n]
            ob, db = obanks[m], dbanks[m]

            def mm(e):
                e.matmul(pb[ob][:, q0:512], lhsT=VV[slot][:, j, :], rhs=PT[pi][:, q0:512], start=(j == 0), stop=(j == nblk - 1))
                return e.matmul(pb[db][:, q0:512], lhsT=onesB[:], rhs=PT[pi][:, q0:512], start=(j == 0), stop=(j == nblk - 1))
            pe(mm, [t_V[slot][j // 4], t_PT[pi], t_const], [t_pb[ob], t_pb[db]])

        for n in range(len(items) + 1):
            if n < len(items):
                stage_s(n)
            if n >= 1:
                stage_pv(n - 1)
            yield

    def fox_head(b, slot, h, bs):
        for g in range(NG):
            yield from fox_group(b, slot, h, bs, g)

    def fox_group(b, slot, h, bs, g):
        ob, db = 4 + g % 2, 6 + g % 2
        if True:
            def exp_emit(m, j, g, r, q0, sbank, pt, rr, ww):
                def em(e):
                    ins = None
                    for u in range(max(r, 0), 4):
                        i = 4 * g + u
                        ins = e.activation(out=pt[:, u * 128:(u + 1) * 128], in_=sbank[:, u * 128:(u + 1) * 128], func=AF.Exp,
                                           bias=BIASF[bs][:, j, i:i + 1])
                    return ins
                act(em, rr, ww)
            yield from softmax_attn(b, slot, g, [(lambda j: KT[slot][:, j * 128:(j + 1) * 128], lambda g, q0: QT[slot][:, g * 512 + q0:(g + 1) * 512])],
                         exp_emit, [ob], [db], [t_BIASF[bs]])
            cs = slice(g * 512, (g + 1) * 512)
            act(lambda e: e.activation(out=TMP[0], in_=pb[db][:], func=AF.Ln), [t_pb[db]], [t_TMP[0]])
            act(lambda e: e.activation(out=TMP[0], in_=TMP[0], func=AF.Exp, scale=-1.0), [t_TMP[0]], [t_TMP[0]])
            dve(lambda e: e.tensor_tensor(out=TMP[1], in0=pb[ob][:], in1=TMP[0], op=ALU.mult), [t_pb[ob], t_TMP[0]], [t_TMP[1]])
            dve(lambda e, cs=cs: e.tensor_tensor(out=OT[:, b, cs], in0=TMP[1], in1=OT[:, b, cs], op=ALU.mult), [t_TMP[1], t_OT[b][g]], [t_OT[b][g]])

    def diff_head(b, slot, d):
        for g in range(NG):
            yield from diff_group(b, slot, d, g)

    def diff_group(b, slot, d, g):
        if True:
            def exp_emit(m, j, g, r, q0, sbank, pt, rr, ww):
                def em(e):
                    ins = None
                    for u2 in range(2):
                        lo = max(q0, 256 * u2)
                        hi = 256 * (u2 + 1)
                        if lo >= hi:
                            continue
                        gi = 2 * g + u2
                        ins = e.activation(out=pt[:, lo:hi], in_=sbank[:, lo:hi], func=AF.Exp, bias=biasD[:, d, j, gi:gi + 1])
                    return ins
                act(em, rr, ww)
            maps = [(lambda j, c=c: KT[slot][c * 64:(c + 1) * 64, j * 128:(j + 1) * 128],
                     lambda g, q0, c=c: QT[slot][c * 64:(c + 1) * 64, g * 512 + q0:(g + 1) * 512]) for c in range(2)]
            yield from softmax_attn(b, slot, g, maps, exp_emit, [4, 5], [6, 7], [t_const])
            cs = slice(g * 512, (g + 1) * 512)
            dve(lambda e: e.tensor_copy(out=TMP[1], in_=pb[4][:]), [t_pb[4]], [t_TMP[1]])
            act(lambda e: e.activation(out=TMP[0], in_=pb[6][:], func=AF.Ln), [t_pb[6]], [t_TMP[0]])
            dve(lambda e: e.tensor_copy(out=TMP[2], in_=pb[5][:]), [t_pb[5]], [t_TMP[2]])
            act(lambda e: e.activation(out=TMP[3], in_=pb[7][:], func=AF.Ln), [t_pb[7]], [t_TMP[3]])
            act(lambda e: e.activation(out=TMP[0], in_=TMP[0], func=AF.Exp, scale=-1.0), [t_TMP[0]], [t_TMP[0]])
            act(lambda e: e.activation(out=TMP[3], in_=TMP[3], func=AF.Exp, scale=-1.0), [t_TMP[3]], [t_TMP[3]])
            dve(lambda e: e.tensor_tensor(out=TMP[1], in0=TMP[1], in1=TMP[0], op=ALU.mult), [t_TMP[1], t_TMP[0]], [t_TMP[1]])
            dve(lambda e: e.tensor_tensor(out=TMP[2], in0=TMP[2], in1=TMP[3], op=ALU.mult), [t_TMP[2], t_TMP[3]], [t_TMP[2]])
            dve(lambda e: e.scalar_tensor_tensor(out=TMP[1], in0=TMP[2], scalar=lamt[:, 4:5], in1=TMP[1], op0=ALU.mult, op1=ALU.add),
                [t_TMP[2], t_TMP[1], t_lam], [t_TMP[1]])
            dve(lambda e: e.tensor_tensor(out=TMP[2], in0=TMP[1], in1=TMP[1], op=ALU.mult), [t_TMP[1]], [t_TMP[2]])
            pe(lambda e: e.matmul(pb[2][:], lhsT=meanF[:], rhs=TMP[2], start=True, stop=True), [t_TMP[2], t_const], [t_pb[2]])
            act(lambda e: e.activation(out=TMP[3], in_=pb[2][:], func=AF.Ln, bias=float(SUBLN_EPS)), [t_pb[2]], [t_TMP[3]])
            act(lambda e: e.activation(out=TMP[3], in_=TMP[3], func=AF.Exp, scale=-0.5), [t_TMP[3]], [t_TMP[3]])
            dve(lambda e: e.tensor_tensor(out=TMP[1], in0=TMP[1], in1=TMP[3], op=ALU.mult), [t_TMP[1], t_TMP[3]], [t_TMP[1]])
            dve(lambda e, cs=cs: e.scalar_tensor_tensor(out=OT[:, b, cs], in0=TMP[1], scalar=gcol[:, 0:1], in1=OT[:, b, cs],
                                                       op0=ALU.mult, op1=ALU.mult), [t_TMP[1], t_lam, t_OT[b][g]], [t_OT[b][g]])

    def sb_head(b, slot):
        for g in range(NG):
            yield from sb_group(b, slot, g)

    def sb_group(b, slot, g):
        ob = 4 + g % 2
        if True:
            nblk = 4 * g + 4
            js = list(range(nblk - 1, -1, -1))
            st = {}
            dve(lambda e: e.memset(ACC32, 0.0), (), [t_ACC32])
            pe(lambda e: e.matmul(pb[ob][:], lhsT=zerosB[:], rhs=QT[slot][:, g * 512:(g + 1) * 512], start=True, stop=False),
               [t_const, t_QT[slot][g]], [t_pb[ob]])

            def stage_z(n):
                j = js[n]
                r, q0 = blk_geom(g, j)
                zb = 2 + n % 2
                si = n % 2
                st[n] = (zb, si)
                kt = KT[slot][:, j * 128:(j + 1) * 128]
                qt = QT[slot][:, g * 512 + q0:(g + 1) * 512]
                def mmz(e):
                    ins = e.matmul(pb[zb][:, q0:512], lhsT=kt, rhs=qt, start=True, stop=(r < 0))
                    if r >= 0:
                        ins = e.matmul(pb[zb][:, q0:q0 + 128], lhsT=identB[:], rhs=maskS[:], start=False, stop=True)
                    return ins
                pe(mmz, [t_KT[slot][j // 4], t_QT[slot][g], t_const], [t_pb[zb]])
                act(lambda e: e.activation(out=pb[zb][:, q0:512], in_=pb[zb][:, q0:512], func=AF.Exp), [t_pb[zb]], [t_pb[zb]])
                act(lambda e: e.activation(out=SP_[si][:, q0:512], in_=pb[zb][:, q0:512], func=AF.Ln, bias=1.0), [t_pb[zb]], [t_SP[si]])

            def stage_arg(n):
                j = js[n]
                r, q0 = blk_geom(g, j)
                zb, si = st[n]
                ab = 6 + n % 2
                ptc[0] += 1
                pi = ptc[0] % 3
                st[n] = (zb, si, pi)
                kt = KT[slot][:, j * 128:(j + 1) * 128]
                qt = QT[slot][:, g * 512 + q0:(g + 1) * 512]
                rd = [t_KT[slot][j // 4], t_QT[slot][g], t_SP[si], t_const]
                if n > 0:
                    rd.append(t_ACCB[(n - 1) % 2])

                def mm(e):
                    e.matmul(pb[ab][:, q0:512], lhsT=kt, rhs=qt, start=True, stop=False)
                    if r >= 0:
                        e.matmul(pb[ab][:, q0:q0 + 128], lhsT=identB[:], rhs=maskS[:], start=False, stop=False)
                    ins = e.matmul(pb[ab][:, q0:512], lhsT=negTriB[:], rhs=SP_[si][:, q0:512], start=False, stop=(n == 0))
                    if n > 0:
                        ins = e.matmul(pb[ab][:, q0:512], lhsT=negOnesB[:], rhs=ACCB[(n - 1) % 2][:, q0:512], start=False, stop=True)
                    return ins
                pe(mm, rd, [t_pb[ab]])
                if n < nblk - 1:
                    dve(lambda e: e.tensor_tensor(out=ACC32[:, q0:512], in0=ACC32[:, q0:512], in1=SP_[si][:, q0:512], op=ALU.add),
                        [t_SP[si], t_ACC32], [t_ACC32])
                    dve(lambda e: e.tensor_copy(out=ACCB[n % 2], in_=ACC32), [t_ACC32], [t_ACCB[n % 2]])
                act(lambda e: e.activation(out=PT[pi][:, q0:512], in_=pb[ab][:, q0:512], func=AF.Exp), [t_pb[ab]], [t_PT[pi]])

            def stage_pv(n):
                j = js[n]
                r, q0 = blk_geom(g, j)
                zb, si, pi = st[n]
                last = (n == nblk - 1)
                pe(lambda e: e.matmul(pb[ob][:, q0:512], lhsT=VV[slot][:, j, :], rhs=PT[pi][:, q0:512], start=False, stop=last),
                   [t_V[slot][j // 4], t_PT[pi]], [t_pb[ob]])

            for n in range(nblk + 2):
                if n < nblk:
                    stage_z(n)
                if 0 <= n - 1 < nblk:
                    stage_arg(n - 1)
                if 0 <= n - 2 < nblk:
                    stage_pv(n - 2)
                yield
            cs = slice(g * 512, (g + 1) * 512)
            dve(lambda e, cs=cs: e.tensor_tensor(out=OT[:, b, cs], in0=pb[ob][:], in1=OT[:, b, cs], op=ALU.mult), [t_pb[ob], t_OT[b][g]], [t_OT[b][g]])

    all_qkv = [t for s_ in range(2) for g_ in range(NG) for t in (t_QT[s_][g_], t_KT[s_][g_], t_V[s_][g_])]
    all_misc = t_PT + t_SP + t_ACCB + [t_ACC32] + t_BIASF + t_TMP + t_GT

    def phase3_prologue(l):
        for cb in range(NCB):
            dma("pool", s_wo[cb], wo[cb], wout_d[l, cb], (), [t_wo[cb]] + t_xT)

    def phase3(l, src, dst):
        def emit(e):
            e.dma_start(out=lng_b, in_=lng_d[l:l + 1, :].partition_broadcast(128)).then_inc(s_lnp, 16)
            return e.dma_start(out=lnb_b, in_=lnb_d[l:l + 1, :].partition_broadcast(128)).then_inc(s_lnp, 16)
        sch.add("sp", emit, (), [t_lnp] + all_misc, dma_sem=s_lnp, ndma=2)
        FM = 512
        nch = (D + FM - 1) // FM
        while D % nch:
            nch += 1
        chw = D // nch
        def zload(t):
            zi = t % 3
            dma("sp", s_zin[zi], zsl[zi], src[t * 128:(t + 1) * 128, :], (), [t_z[zi]] + (all_qkv if t < 3 else []))
        for t in range(min(3, NT)):
            zload(t)
        for t in range(NT):
            zi = t % 3
            z = zsl[zi]
            yb = [(t % 2) * NCB + cb for cb in range(NCB)]

            def mm(e, t=t, yb=yb):
                ins = None
                for kc in range(KC):
                    for cb in range(NCB):
                        ins = e.matmul(pb[yb[cb]][:, 0:CBW], lhsT=OT[:, kc, t * 128:(t + 1) * 128], rhs=wo[cb][:, kc, :],
                                       start=(kc == 0), stop=(kc == KC - 1))
                return ins
            pe(mm, t_wo + [t_OT[b][t // 4] for b in range(NH)], [t_pb[i] for i in yb])
            for cb in range(NCB):
                dve(lambda e, cb=cb, z=z, yb=yb: e.scalar_tensor_tensor(out=z[:, cb * CBW:(cb + 1) * CBW], in0=z[:, cb * CBW:(cb + 1) * CBW],
                                                                      scalar=float(DEEPNORM_ALPHA), in1=pb[yb[cb]][:, 0:CBW],
                                                                      op0=ALU.mult, op1=ALU.add), [t_pb[yb[cb]], t_z[zi]], [t_z[zi]])
            for c in range(nch):
                dve(lambda e, c=c, z=z: e.bn_stats(out=bnst[:, c, :], in_=z[:, c * chw:(c + 1) * chw]), [t_z[zi], t_st], [t_st])
            dve(lambda e: e.bn_aggr(out=mv[:, 0:2], in_=bnst[:, 0:nch, :]), [t_st], [t_st])
            act(lambda e: e.activation(out=mv[:, 2:3], in_=mv[:, 1:2], func=AF.Ln, bias=float(LN_EPS)), [t_st], [t_st])
            act(lambda e: e.activation(out=mv[:, 3:4], in_=mv[:, 2:3], func=AF.Exp, scale=-0.5), [t_st], [t_st])
            dve(lambda e: e.scalar_tensor_tensor(out=mv[:, 4:5], in0=mv[:, 0:1], scalar=-1.0, in1=mv[:, 3:4], op0=ALU.mult, op1=ALU.mult),
                [t_st], [t_st])
            act(lambda e, z=z: e.activation(out=z, in_=z, func=AF.Identity, scale=mv[:, 3:4], bias=mv[:, 4:5]), [t_st, t_z[zi]], [t_z[zi], t_st])
            pool(lambda e, z=z: e.tensor_tensor(out=z, in0=z, in1=lng_b, op=ALU.mult), [t_z[zi], t_lnp], [t_z[zi]])
            pool(lambda e, z=z: e.tensor_tensor(out=z, in0=z, in1=lnb_b, op=ALU.add), [t_z[zi], t_lnp], [t_z[zi]])
            dma("sp", s_zout[zi], dst[t * 128:(t + 1) * 128, :], z, [t_z[zi]], ())
            if t + 3 < NT:
                zload(t + 3)

    stop = cfg.get("stop")
    consts()
    sch.barrier()
    for sq in range(NSEQ):
        for l in range(DEPTH):
            if stop == "consts":
                break
            src = x_d[sq] if l == 0 else scr_d[(l - 1) % 2]
            dst = out_d[sq] if l == DEPTH - 1 else scr_d[l % 2]
            phase1(src)
            if cfg.get("p1bar", False):
                sch.barrier()
            if stop == "p1":
                break
            load_params(l)
            if stop == "par":
                break
            def drain(gen):
                for _ in gen:
                    pass

            def interleave(main, side, n_main, n_side):
                n_main = max(1, int(n_main * 0.85))
                i = 0
                emitted = 0
                done = False
                for _ in main:
                    want = min(n_side, (i * n_side) // n_main + 1)
                    while not done and emitted < want:
                        try:
                            next(side)
                            emitted += 1
                        except StopIteration:
                            done = True
                    i += 1
                if not done:
                    drain(side)

            def attn_gen(b, slot):
                if b < NF:
                    yield from fox_head(b, slot, b, b % 2)
                elif b < NF + NS:
                    yield from sb_head(b, slot)
                else:
                    yield from diff_head(b, slot, b - NF - NS)

            def n_attn(b):
                base = sum(4 * g + 5 for g in range(NG))
                if b < NF:
                    return base
                if b < NF + NS:
                    return base + NG
                return 2 * base - NG

            def kind(b):
                return "f" if b < NF else ("s" if b < NF + NS else "d")

            if stop in ("inproj", "fbias"):
                for b in range(NH):
                    drain(inproj_head(l, b, 0, "f"))
                    if stop == "fbias":
                        fox_bias(b, b % 2)
                break
            prev = None
            for b in range(NH):
                ip = inproj_head(l, b, b % 2, kind(b))
                if prev is None:
                    def prep_gen():
                        if NF > 0:
                            yield from fox_prep(l)
                    interleave(ip, prep_gen(), 4 * NG, 4)
                    if NF > 0:
                        fox_bias(0, 0)
                else:
                    if b < NF:
                        fox_bias(b, b % 2)
                    interleave(prev, ip, n_attn(b - 1), 4 * NG)
                prev = attn_gen(b, b % 2)
            if stop == "attn":
                drain(prev)
                break

            def p3pro():
                phase3_prologue(l)
                yield
            interleave(prev, p3pro(), 1, 1)
            phase3(l, src, dst)
            sch.barrier()
    sch.finish()
    sch.emit_all(nc, es)
    es.close()
    return nc


def layout_weights(cfg, w_in, w_out):
    NF, NS, ND = cfg["NF"], cfg["NS"], cfg["ND"]
    NH = NF + NS + ND
    D = 128 * NH
    KC = NH
    DEPTH = w_in.shape[0]
    FW, SW, DW = NF * 128, NS * 128, ND * 128
    offs = []
    for h in range(NF):
        offs.append([k * FW + h * 128 for k in range(4)])
    base = 4 * FW
    for h in range(NS):
        offs.append([base + k * SW + h * 128 for k in range(4)])
    base = 4 * FW + 4 * SW
    for h in range(ND):
        offs.append([base + k * DW + h * 128 for k in range(4)])
    ffo = 4 * FW + 4 * SW + 4 * DW
    w_in_r = np.empty((DEPTH, NH, 4, 128, KC, 128), np.float32)
    for b in range(NH):
        for c in range(4):
            blk = w_in[:, :, offs[b][c]:offs[b][c] + 128]
            w_in_r[:, b, c] = blk.reshape(DEPTH, KC, 128, 128).transpose(0, 2, 1, 3)
    NFp = max(NF, 1)
    w_ff_r = np.zeros((DEPTH, 128, KC, NFp), np.float32)
    if NF > 0:
        w_ff_r[:] = w_in[:, :, ffo:ffo + NF].reshape(DEPTH, KC, 128, NF).transpose(0, 2, 1, 3)
    CBW = min(512, D)
    NCB = D // CBW
    w_out_r = np.ascontiguousarray(
        w_out.reshape(DEPTH, KC, 128, NCB, CBW).transpose(0, 3, 2, 1, 4))
    return w_in_r, w_ff_r, w_out_r


def run(cfg, x, w_in, b_f, diff_lambda, diff_subln_g, w_out, ln_g, ln_b, n_cores):
    NSEQ = cfg["NSEQ"]
    DEPTH = cfg["DEPTH"]
    x = np.ascontiguousarray(np.asarray(x, np.float32))
    w_in_r, w_ff_r, w_out_r = layout_weights(cfg, np.asarray(w_in, np.float32), np.asarray(w_out, np.float32))
    NFp = max(cfg["NF"], 1)
    bfp = np.zeros((DEPTH, NFp), np.float32)
    if cfg["NF"] > 0:
        bfp[:] = np.asarray(b_f, np.float32)
    common = {
        "w_in_r": w_in_r, "w_ff_r": w_ff_r, "w_out_r": w_out_r, "b_f": bfp,
        "diff_lambda": np.ascontiguousarray(np.asarray(diff_lambda, np.float32).reshape(DEPTH, 256)),
        "diff_subln_g": np.ascontiguousarray(np.asarray(diff_subln_g, np.float32)),
        "ln_g": np.ascontiguousarray(np.asarray(ln_g, np.float32)),
        "ln_b": np.ascontiguousarray(np.asarray(ln_b, np.float32)),
    }
    nc = build_nc(cfg)
    in_maps = []
    for c in range(n_cores):
        m = dict(common)
        m["x"] = np.ascontiguousarray(x[c * NSEQ:(c + 1) * NSEQ])
        in_maps.append(m)
    res = run_bass_kernel_spmd(nc, in_maps, core_ids=list(range(n_cores)))
    return np.concatenate([np.asarray(r["out"], np.float32) for r in res.results], axis=0)


def kernel(x, w_in, b_f, diff_lambda, diff_subln_g, w_out, ln_g, ln_b):
    return run(FULL_CFG, x, w_in, b_f, diff_lambda, diff_subln_g, w_out, ln_g, ln_b, 8)
```

```python
import math
from contextlib import ExitStack

import numpy as np
import concourse.bass as bass
import concourse.mybir as mybir
from concourse.bass_utils import run_bass_kernel_spmd

F32 = mybir.dt.float32
BF16 = mybir.dt.bfloat16
AF = mybir.ActivationFunctionType
ALU = mybir.AluOpType
AX = mybir.AxisListType

FULL_CFG = dict(S=2048, NF=6, NS=6, ND=4, DEPTH=4, NSEQ=2)
DEEPNORM_ALPHA = (2 * 4) ** 0.25
LN_EPS = 1e-5
SUBLN_EPS = 1e-5
STREAMS = ("pe", "act", "dve", "pool", "sp")
MAXC = 20000
SAME_ENGINE_RAW = True


class Tile:
    __slots__ = ("name", "w", "r")

    def __init__(self, name):
        self.name = name
        self.w = None
        self.r = []


class Op:
    __slots__ = ("stream", "idx", "emit", "deps", "dma_sem", "dma_val", "signal", "signo", "raw_same")

    def __init__(self, stream, idx, emit):
        self.stream = stream
        self.idx = idx
        self.emit = emit
        self.deps = []
        self.dma_sem = None
        self.dma_val = 0
        self.signal = False
        self.signo = 0
        self.raw_same = False


class Sched:
    def __init__(self):
        self.streams = {s: [] for s in STREAMS}
        self.pending = {s: [] for s in STREAMS}
        self.dma_since_barrier = []
        self.dma_counts = {}

    def add(self, stream, emit, reads=(), writes=(), dma_sem=None, ndma=1):
        op = Op(stream, len(self.streams[stream]), emit)
        deps = {}

        def dep(o, raw):
            if o is None:
                return
            k = id(o)
            if k in deps:
                deps[k] = (o, deps[k][1] or raw)
            else:
                deps[k] = (o, raw)

        for t in reads:
            dep(t.w, True)
        for t in writes:
            dep(t.w, False)
            for o in t.r:
                dep(o, False)
        for o in self.pending[stream]:
            dep(o, True)
        self.pending[stream] = []
        best = {}
        out = []
        for o, raw in deps.values():
            if o.dma_sem is not None:
                out.append((o, raw))
            else:
                b = best.get(o.stream)
                if b is None or o.idx > b[0].idx:
                    best[o.stream] = (o, raw)
                elif o.idx == b[0].idx and raw:
                    best[o.stream] = (o, True)
        for s, (o, raw) in best.items():
            if s == stream and o.dma_sem is None:
                if stream == "pe" or stream == "sp":
                    continue
            out.append((o, raw))
        op.deps = [o for o, _ in out]
        for o in op.deps:
            o.signal = True
        if dma_sem is not None:
            op.dma_sem = dma_sem
            c = self.dma_counts.get(id(dma_sem), 0) + ndma
            self.dma_counts[id(dma_sem)] = c
            op.dma_val = 16 * c
            self.dma_since_barrier.append(op)
        for t in reads:
            t.r.append(op)
            if len(t.r) > 24:
                keep = {}
                dm = []
                for o in t.r:
                    if o.dma_sem is not None:
                        dm.append(o)
                    else:
                        k = keep.get(o.stream)
                        if k is None or o.idx > k.idx:
                            keep[o.stream] = o
                t.r = list(keep.values()) + dm[-16:]
                if len(dm) > 16:
                    t.r = list(keep.values()) + dm
        for t in writes:
            t.w = op
            t.r = []
        self.streams[stream].append(op)
        return op

    def barrier(self):
        lasts = [self.streams[s][-1] for s in STREAMS if self.streams[s]]
        for s in STREAMS:
            self.pending[s] = self.pending[s] + list(lasts) + list(self.dma_since_barrier)
        self.dma_since_barrier = []

    def finish(self):
        self.barrier()
        self.add("sp", None)

    def emit_all(self, nc, es):
        nsig = {}
        for s in STREAMS:
            n = 0
            for op in self.streams[s]:
                if op.signal and op.dma_sem is None:
                    n += 1
                    op.signo = n
            nsig[s] = n
        sems = {}
        for s in STREAMS:
            nb = max(1, (nsig[s] + MAXC - 1) // MAXC)
            sems[s] = [es.enter_context(nc.semaphore(f"sem_{s}_{i}")) for i in range(nb)]
        block = es.enter_context(nc.Block())

        def run(stream, eng):
            waited = {s: 0 for s in STREAMS}
            dwaited = {}
            for op in self.streams[stream]:
                for d in op.deps:
                    if d.dma_sem is not None:
                        k = id(d.dma_sem)
                        if dwaited.get(k, 0) >= d.dma_val:
                            continue
                        eng.wait_ge(d.dma_sem, d.dma_val)
                        dwaited[k] = d.dma_val
                    else:
                        if waited[d.stream] >= d.signo:
                            continue
                        n = d.signo - 1
                        eng.wait_ge(sems[d.stream][n // MAXC], (n % MAXC) + 1)
                        waited[d.stream] = d.signo
                if op.emit is None:
                    continue
                ins = op.emit(eng)
                if op.dma_sem is not None:
                    pass
                elif op.signal:
                    n = op.signo - 1
                    ins.then_inc(sems[stream][n // MAXC], 1)

        @block.tensor
        def _(e):
            run("pe", e)

        @block.scalar
        def _(e):
            run("act", e)

        @block.vector
        def _(e):
            run("dve", e)

        @block.gpsimd
        def _(e):
            run("pool", e)

        @block.sync
        def _(e):
            run("sp", e)


def build_nc(cfg):
    S, NF, NS, ND, DEPTH, NSEQ = (cfg[k] for k in ("S", "NF", "NS", "ND", "DEPTH", "NSEQ"))
    NH = NF + NS + ND
    D = 128 * NH
    KC = NH
    NT = S // 128
    NG = S // 512
    CBW = min(512, D)
    NCB = D // CBW
    assert NCB * CBW == D and NCB <= 4
    NWS = 5
    NFp = max(NF, 1)

    nc = bass.Bass("TRN2", target_bir_lowering=False)
    x_d = nc.dram_tensor("x", [NSEQ, S, D], F32, kind="ExternalInput").ap()
    win_d = nc.dram_tensor("w_in_r", [DEPTH, NH, 4, 128, KC, 128], F32, kind="ExternalInput").ap()
    wff_d = nc.dram_tensor("w_ff_r", [DEPTH, 128, KC, NFp], F32, kind="ExternalInput").ap()
    wout_d = nc.dram_tensor("w_out_r", [DEPTH, NCB, 128, KC, CBW], F32, kind="ExternalInput").ap()
    bf_d = nc.dram_tensor("b_f", [DEPTH, NFp], F32, kind="ExternalInput").ap()
    dl_d = nc.dram_tensor("diff_lambda", [DEPTH, 256], F32, kind="ExternalInput").ap()
    sg_d = nc.dram_tensor("diff_subln_g", [DEPTH, 128], F32, kind="ExternalInput").ap()
    lng_d = nc.dram_tensor("ln_g", [DEPTH, D], F32, kind="ExternalInput").ap()
    lnb_d = nc.dram_tensor("ln_b", [DEPTH, D], F32, kind="ExternalInput").ap()
    out_d = nc.dram_tensor("out", [NSEQ, S, D], F32, kind="ExternalOutput").ap()
    scr_d = nc.dram_tensor("scr", [2, S, D], F32, kind="Internal").ap()

    sch = Sched()
    es = ExitStack()
    E = es.enter_context

    def sb(name, shape, dt):
        return E(nc.sbuf_tensor(name, shape, dt))

    def dsem(name):
        return E(nc.semaphore(name))

    RSZ = max(KC * S, NCB * KC * CBW)
    R = sb("R", [128, RSZ], BF16)
    xT = R[:, 0:KC * S].rearrange("p (k s) -> p k s", k=KC)
    wo = [R[:, cb * KC * CBW:(cb + 1) * KC * CBW].rearrange("p (k c) -> p k c", k=KC) for cb in range(NCB)]
    OTSZ = max(NH * S, 2 * 2 * D + 2 * D)
    OTr = sb("OT", [128, OTSZ], BF16)
    OT = OTr[:, 0:NH * S].rearrange("p (h s) -> p h s", h=NH)
    NXB = 4
    xb = [OTr[:, i * D:(i + 1) * D] for i in range(NXB)]
    QKVSZ = max(2 * 3 * S, 3 * 2 * D)
    QKVr = sb("QKV", [128, QKVSZ], BF16)
    QT = [QKVr[:, (3 * s + 0) * S:(3 * s + 1) * S] for s in range(2)]
    KT = [QKVr[:, (3 * s + 1) * S:(3 * s + 2) * S] for s in range(2)]
    VV = [QKVr[:, (3 * s + 2) * S:(3 * s + 3) * S].rearrange("p (t d) -> p t d", d=128) for s in range(2)]
    zsl = [QKVr[:, i * 2 * D:(i + 1) * 2 * D].bitcast(F32) for i in range(3)]
    Wsl = [sb(f"W{i}", [128, KC, 128], BF16) for i in range(NWS)]
    wff = sb("wff", [128, KC, NFp], BF16)
    MISC_P2 = 3 * 512 + 2 * 512 + 2 * 512 + 2 * 512
    MISC_F32 = 2 * NT * NT + 6 * 512
    MSZ = max(MISC_P2 + 2 * MISC_F32, 2 * 2 * D)
    Mr = sb("MISC", [128, MSZ], BF16)
    off = 0

    def carve(n, dt=BF16):
        nonlocal off
        if dt == F32:
            a = Mr[:, off:off + 2 * n].bitcast(F32)
            off += 2 * n
        else:
            a = Mr[:, off:off + n]
            off += n
        return a

    PT = [carve(512) for _ in range(3)]
    SP_ = [carve(512) for _ in range(2)]
    ACCB = [carve(512) for _ in range(2)]
    ACC32 = carve(512, F32)
    BIASF = [carve(NT * NT, F32).rearrange("p (j i) -> p j i", j=NT) for _ in range(2)]
    TMP = [carve(512, F32) for _ in range(4)]
    GT = [carve(512, F32) for _ in range(2)]
    assert off <= MSZ
    lng_b = Mr[:, 0:2 * D].bitcast(F32)
    lnb_b = Mr[:, 2 * D:4 * D].bitcast(F32)

    identB = sb("identB", [128, 128], BF16)
    onesB = sb("onesB", [128, 128], BF16)
    zerosB = sb("zerosB", [128, 128], BF16)
    negOnesB = sb("negOnesB", [128, 128], BF16)
    negTriB = sb("negTriB", [128, 128], BF16)
    maskC = sb("maskC", [128, 128], BF16)
    maskS = sb("maskS", [128, 128], BF16)
    negF = TMP[0][:, 0:128]
    cF = TMP[1][:, 0:128]
    bigF = TMP[2][:, 0:128]
    onesF = sb("onesF", [128, 128], F32)
    triuF = sb("triuF", [128, 128], F32)
    sel64F = sb("sel64F", [128, 128], F32)
    meanF = sb("meanF", [128, 128], F32)
    tokF = sb("tokF", [128, NT], F32)
    NGI = 2 * NG
    biasD = sb("biasD", [128, max(ND, 1), NT, NGI], F32)
    bfb = sb("bfb", [128, NFp], F32)
    dlb = sb("dlb", [128, 256], F32)
    sgc = sb("sgc", [128, 1], F32)
    gcol = sb("gcol", [128, 1], F32)
    lamt = sb("lamt", [128, 8], F32)
    dtmp = sb("dtmp", [128, 64], F32)
    nl = sb("nl", [128, NT, NFp], F32)
    Tsb = sb("Tsb", [128, NT, NFp], F32)
    pre = sb("pre", [128, NT, NFp], F32)
    ncol = sb("ncol", [128, NT, NFp], F32)
    midsb = sb("midsb", [128, NT, NFp], F32)
    bnst = sb("bnst", [128, 8, 6], F32)
    mv = sb("mv", [128, 8], F32)

    pb = [E(nc.psum_tensor(f"pb{i}", [128, 512], F32)) for i in range(8)]

    T = Tile
    t_xT = [T(f"xT{t}") for t in range(NT)]
    t_wo = [T(f"wo{c}") for c in range(NCB)]
    t_OT = [[T(f"OT{b}_{g}") for g in range(NG)] for b in range(NH)]
    t_xb = [T(f"xb{i}") for i in range(NXB)]
    t_QT = [[T(f"QT{s}_{g}") for g in range(NG)] for s in range(2)]
    t_KT = [[T(f"KT{s}_{g}") for g in range(NG)] for s in range(2)]
    t_V = [[T(f"V{s}_{g}") for g in range(NG)] for s in range(2)]
    t_z = [T(f"z{i}") for i in range(3)]
    t_W = [T(f"W{i}") for i in range(NWS)]
    t_wff = T("wff")
    t_PT = [T(f"PT{i}") for i in range(3)]
    t_SP = [T(f"SP{i}") for i in range(2)]
    t_ACCB = [T(f"ACCB{i}") for i in range(2)]
    t_ACC32 = T("ACC32")
    t_BIASF = [T("BIASF0"), T("BIASF1")]
    t_TMP = [T(f"TMP{i}") for i in range(4)]
    t_GT = [T("GT0"), T("GT1")]
    t_lnp = T("lnp")
    t_const = T("const")
    t_par = T("par")
    t_lam = T("lam")
    t_fox = T("foxprep")
    t_st = T("stats")
    t_pb = [T(f"pb{i}") for i in range(8)]
    t_dram = T("dram")

    s_xb = [dsem(f"s_xb{i}") for i in range(NXB)]
    s_zin = [dsem(f"s_zin{i}") for i in range(3)]
    s_zout = [dsem(f"s_zout{i}") for i in range(3)]
    s_W = [dsem(f"s_W{i}") for i in range(NWS)]
    s_wff = dsem("s_wff")
    s_wo = [dsem(f"s_wo{i}") for i in range(NCB)]
    s_lnp = dsem("s_lnp")
    s_par = dsem("s_par")

    pe = lambda emit, r=(), w=(): sch.add("pe", emit, r, w)
    act = lambda emit, r=(), w=(): sch.add("act", emit, r, w)
    dve = lambda emit, r=(), w=(): sch.add("dve", emit, r, w)
    pool = lambda emit, r=(), w=(): sch.add("pool", emit, r, w)

    pool_dmas = []
    MAX_SWDGE_INFLIGHT = 3

    def dma(stream, sem, out, in_, r=(), w=()):
        def emit(e):
            return e.dma_start(out=out, in_=in_).then_inc(sem, 16)
        if stream == "pool":
            if len(pool_dmas) >= MAX_SWDGE_INFLIGHT:
                sch.pending["pool"] = sch.pending["pool"] + [pool_dmas[-MAX_SWDGE_INFLIGHT]]
        op = sch.add(stream, emit, r, w, dma_sem=sem)
        if stream == "pool":
            pool_dmas.append(op)
        return op

    def consts():
        pool(lambda e: e.memset(onesF[:], 1.0), (), [t_const])
        pool(lambda e: e.memset(negF, -1.0), (), [t_const])
        pool(lambda e: e.memset(meanF[:], 1.0 / 128.0), (), [t_const])
        pool(lambda e: e.memset(zerosB[:], 0.0), (), [t_const])
        pool(lambda e: e.memset(onesB[:], 1.0), (), [t_const])
        pool(lambda e: e.memset(negOnesB[:], -1.0), (), [t_const])

        def sel(out, in_, pat, cm, base, op):
            pool(lambda e: e.affine_select(out=out, in_=in_, pattern=pat, compare_op=op, fill=0.0,
                                           base=base, channel_multiplier=cm), [t_const], [t_const])

        sel(triuF[:], onesF[:], [[1, 128]], -1, 0, ALU.is_ge)
        sel(sel64F[:], onesF[:], [[0, 128]], 1, -64, ALU.is_equal)
        sel(cF, onesF[:], [[1, 128]], -1, 0, ALU.is_equal)
        pool(lambda e: e.tensor_copy(out=identB[:], in_=cF), [t_const], [t_const])
        pool(lambda e: e.memset(bigF, -30000.0), (), [t_const])
        sel(cF, bigF, [[-1, 128]], 1, -1, ALU.is_ge)
        pool(lambda e: e.tensor_copy(out=maskC[:], in_=cF), [t_const], [t_const])
        sel(cF, bigF, [[-1, 128]], 1, 0, ALU.is_ge)
        pool(lambda e: e.tensor_copy(out=maskS[:], in_=cF), [t_const], [t_const])
        sel(cF, negF, [[-1, 128]], 1, 0, ALU.is_ge)
        pool(lambda e: e.tensor_copy(out=negTriB[:], in_=cF), [t_const], [t_const])
        pool(lambda e: e.iota(tokF[:], pattern=[[128, NT]], base=0, channel_multiplier=1,
                              allow_small_or_imprecise_dtypes=True), (), [t_const])
        for d in range(ND):
            slope = 2.0 ** (-8.0 * (d + 1) / ND)
            for gi in range(NGI):
                cen = 256 * gi + 128
                pool(lambda e, d=d, gi=gi, cen=cen, slope=slope: e.tensor_scalar(
                    out=biasD[:, d, :, gi], in0=tokF[:], scalar1=float(-cen), scalar2=float(slope),
                    op0=ALU.add, op1=ALU.mult), [t_const], [t_const])

    evac_rr = [0]

    def evac_copy(out, in_, r, w):
        evac_rr[0] += 1
        if evac_rr[0] % 2:
            dve(lambda e: e.tensor_copy(out=out, in_=in_), r, w)
        else:
            act(lambda e: e.activation(out=out, in_=in_, func=AF.Copy), r, w)

    def phase1(src):
        for t in range(NT):
            sl = t % NXB
            dma("pool", s_xb[sl], xb[sl], src[t * 128:(t + 1) * 128, :], (), [t_xb[sl]])
            for c0 in range(0, KC, 8):
                n = min(8, KC - c0)
                bi = ((t * ((KC + 7) // 8)) + c0 // 8) % 2
                bank = pb[bi][:].bitcast(BF16)

                def tr(e, sl=sl, c0=c0, n=n, bank=bank):
                    ins = None
                    for i in range(n):
                        ins = e.transpose(bank[:, i * 128:(i + 1) * 128], xb[sl][:, (c0 + i) * 128:(c0 + i + 1) * 128], identB[:])
                    return ins
                pe(tr, [t_xb[sl], t_const], [t_pb[bi]])
                evac_copy(xT[:, c0:c0 + n, t * 128:(t + 1) * 128],
                          bank[:, 0:n * 128].rearrange("p (c s) -> p c s", c=n), [t_pb[bi]], [t_xT[t]])

    wlist = [(l, b, c) for _sq in range(NSEQ) for l in range(DEPTH) for b in range(NH) for c in (0, 1, 3, 2)]
    wissued = [0]
    wuse = [0]
    WAHEAD = 2

    def w_prefetch(upto):
        while wissued[0] <= min(upto, len(wlist) - 1):
            n = wissued[0]
            l, b, c = wlist[n]
            i = n % NWS
            dma("pool", s_W[i], Wsl[i][:], win_d[l, b, c], (), [t_W[i]])
            wissued[0] += 1

    def next_w(l, b, c):
        n = wuse[0]
        assert wlist[n] == (l, b, c)
        wuse[0] += 1
        w_prefetch(n + WAHEAD)
        return n % NWS

    ipb = [0]
    gctr = [0]

    def inproj_bank():
        ipb[0] += 1
        return ipb[0] % 2

    def inproj_head(l, b, slot, kind):
        qscale = (64 if kind == "d" else 128) ** -0.5
        wi = {}
        pend = []

        def flush(n):
            for _ in range(n):
                if pend:
                    pend.pop(0)()

        for c in (0, 1, 3):
            wi[c] = next_w(l, b, c)
            for tg in range(NG):
                flush(2)
                bi = inproj_bank()

                def mm(e, bi=bi, w=wi[c], tg=tg):
                    ins = None
                    for kc in range(KC):
                        ins = e.matmul(pb[bi][:], lhsT=Wsl[w][:, kc, :], rhs=xT[:, kc, tg * 512:(tg + 1) * 512],
                                       start=(kc == 0), stop=(kc == KC - 1))
                    return ins
                pe(mm, [t_W[wi[c]]] + t_xT[4 * tg:4 * tg + 4], [t_pb[bi]])
                cs = slice(tg * 512, (tg + 1) * 512)
                if c == 0:
                    dve(lambda e, bi=bi, cs=cs: e.tensor_scalar(out=QT[slot][:, cs], in0=pb[bi][:], scalar1=float(qscale),
                                                                scalar2=None, op0=ALU.mult), [t_pb[bi]], [t_QT[slot][tg]])
                elif c == 1:
                    dve(lambda e, bi=bi, cs=cs: e.tensor_copy(out=KT[slot][:, cs], in_=pb[bi][:]), [t_pb[bi]], [t_KT[slot][tg]])
                else:
                    gi_ = gctr[0] % 2
                    gctr[0] += 1
                    gt = GT[gi_]
                    tgt = t_GT[gi_]
                    ot_w = [t_OT[b][tg]] + (t_xb if b * S < NXB * D else [])
                    dve(lambda e, bi=bi, cs=cs: e.tensor_copy(out=OT[:, b, cs], in_=pb[bi][:]), [t_pb[bi]], ot_w)
                    act(lambda e, cs=cs, gt=gt: e.activation(out=gt, in_=OT[:, b, cs], func=AF.Exp, scale=-1.0), [t_OT[b][tg]], [tgt])
                    pend.append(lambda gt=gt, tgt=tgt: act(lambda e: e.activation(out=gt, in_=gt, func=AF.Ln, bias=1.0), [tgt], [tgt]))
                    pend.append(lambda gt=gt, tgt=tgt: act(lambda e: e.activation(out=gt, in_=gt, func=AF.Exp, scale=-1.0), [tgt], [tgt]))
                    pend.append(lambda gt=gt, tgt=tgt, cs=cs, tg=tg: pool(
                        lambda e: e.tensor_tensor(out=OT[:, b, cs], in0=OT[:, b, cs], in1=gt, op=ALU.mult), [tgt, t_OT[b][tg]], [t_OT[b][tg]]))
                yield
        wi[2] = next_w(l, b, 2)
        for tg in range(NG):
            flush(2)
            bi = inproj_bank()

            def mmv(e, bi=bi, w=wi[2], tg=tg):
                ins = None
                for u in range(4):
                    t = 4 * tg + u
                    for kc in range(KC):
                        ins = e.matmul(pb[bi][:, u * 128:(u + 1) * 128], lhsT=xT[:, kc, t * 128:(t + 1) * 128],
                                       rhs=Wsl[w][:, kc, :], start=(kc == 0), stop=(kc == KC - 1))
                return ins
            pe(mmv, [t_W[wi[2]]] + t_xT[4 * tg:4 * tg + 4], [t_pb[bi]])
            dve(lambda e, bi=bi, tg=tg: e.tensor_copy(out=VV[slot][:, 4 * tg:4 * tg + 4, :],
                                                      in_=pb[bi][:].rearrange("p (u d) -> p u d", u=4)),
                [t_pb[bi]], [t_V[slot][tg]])
            yield
        while pend:
            flush(2)
            yield

    def load_params(l):
        lam_init = 0.8 - 0.6 * math.exp(-0.3 * l)

        def emit(e):
            e.dma_start(out=bfb[:], in_=bf_d[l:l + 1, :].partition_broadcast(128)).then_inc(s_par, 16)
            e.dma_start(out=dlb[:], in_=dl_d[l:l + 1, :].partition_broadcast(128)).then_inc(s_par, 16)
            return e.dma_start(out=sgc[:], in_=sg_d[l:l + 1, :].rearrange("o p -> p o")).then_inc(s_par, 16)
        sch.add("sp", emit, (), [t_par], dma_sem=s_par, ndma=3)
        if ND > 0:
            dve(lambda e: e.tensor_tensor(out=dtmp[:], in0=dlb[:, 0:64], in1=dlb[:, 64:128], op=ALU.mult), [t_par], [t_lam])
            dve(lambda e: e.reduce_sum(out=lamt[:, 0:1], in_=dtmp[:], axis=AX.X), [t_lam], [t_lam])
            dve(lambda e: e.tensor_tensor(out=dtmp[:], in0=dlb[:, 128:192], in1=dlb[:, 192:256], op=ALU.mult), [t_par, t_lam], [t_lam])
            dve(lambda e: e.reduce_sum(out=lamt[:, 1:2], in_=dtmp[:], axis=AX.X), [t_lam], [t_lam])
            act(lambda e: e.activation(out=lamt[:, 2:4], in_=lamt[:, 0:2], func=AF.Exp), [t_lam], [t_lam])
            dve(lambda e: e.scalar_tensor_tensor(out=lamt[:, 4:5], in0=lamt[:, 3:4], scalar=float(-lam_init), in1=lamt[:, 2:3],
                                                 op0=ALU.add, op1=ALU.subtract), [t_lam], [t_lam])
            dve(lambda e: e.tensor_scalar(out=gcol[:], in0=sgc[:], scalar1=float(1.0 - lam_init), scalar2=None, op0=ALU.mult),
                [t_par, t_lam], [t_lam])

    def fox_prep(l):
        NC_ = NT * NF
        dma("pool", s_wff, wff[:], wff_d[l], (), [t_wff])

        def mmf(e):
            ins = None
            for t in range(NT):
                for kc in range(KC):
                    ins = e.matmul(pb[5][:, t * NF:(t + 1) * NF], lhsT=xT[:, kc, t * 128:(t + 1) * 128], rhs=wff[:, kc, 0:NF],
                                   start=(kc == 0), stop=(kc == KC - 1))
            return ins
        pe(mmf, [t_wff] + t_xT, [t_pb[5]])
        yield
        for t in range(NT):
            dve(lambda e, t=t: e.tensor_tensor(out=nl[:, t, 0:NF], in0=pb[5][:, t * NF:(t + 1) * NF], in1=bfb[:, 0:NF], op=ALU.add),
                [t_pb[5], t_par], [t_fox])
        nl2 = nl[:].rearrange("p t f -> p (t f)")
        act(lambda e: e.activation(out=nl2, in_=nl2, func=AF.Exp, scale=-1.0), [t_fox], [t_fox])
        act(lambda e: e.activation(out=nl2, in_=nl2, func=AF.Ln, bias=1.0), [t_fox], [t_fox])
        yield
        pe(lambda e: e.matmul(pb[5][:, 0:NC_], lhsT=onesF[:], rhs=nl2, start=True, stop=True), [t_fox, t_const], [t_pb[5]])
        pe(lambda e: e.matmul(pb[5][:, 128:128 + NC_], lhsT=triuF[:], rhs=nl2, start=True, stop=True), [t_fox, t_const], [t_pb[5]])
        dve(lambda e: e.tensor_copy(out=Tsb[:].rearrange("p t f -> p (t f)"), in_=pb[5][:, 0:NC_]), [t_pb[5]], [t_fox])
        yield
        dve(lambda e: e.memset(pre[:, 0, :], 0.0), [t_fox], [t_fox])
        for j in range(1, NT):
            dve(lambda e, j=j: e.tensor_tensor(out=pre[:, j, :], in0=pre[:, j - 1, :], in1=Tsb[:, j - 1, :], op=ALU.add), [t_fox], [t_fox])
        dve(lambda e: e.tensor_tensor(out=ncol[:].rearrange("p t f -> p (t f)"), in0=pb[5][:, 128:128 + NC_],
                                      in1=pre[:].rearrange("p t f -> p (t f)"), op=ALU.add), [t_pb[5], t_fox], [t_fox])
        pe(lambda e: e.matmul(pb[5][:, 256:256 + NC_], lhsT=sel64F[:], rhs=ncol[:].rearrange("p t f -> p (t f)"), start=True, stop=True),
           [t_fox, t_const], [t_pb[5]])
        dve(lambda e: e.tensor_copy(out=midsb[:].rearrange("p t f -> p (t f)"), in_=pb[5][:, 256:256 + NC_]), [t_pb[5]], [t_fox])

    def fox_bias(h, bs):
        def em(e):
            ins = None
            for i in range(NT):
                ins = e.tensor_scalar(out=BIASF[bs][:, :, i], in0=ncol[:, :, h], scalar1=midsb[:, i, h:h + 1], scalar2=None,
                                      op0=ALU.subtract)
            return ins
        dve(em, [t_fox], [t_BIASF[bs]])

    ptc = [0]
    sbk = [0]

    def blk_geom(g, j):
        r = j - 4 * g
        q0 = 128 * r if r >= 0 else 0
        return r, q0

    def softmax_attn(b, slot, g, maps, exp_emit, obanks, dbanks, extra_r):
        nblk = 4 * g + 4
        items = [(j, m) for j in range(nblk) for m in range(len(maps))]
        st = {}

        def stage_s(n):
            j, m = items[n]
            r, q0 = blk_geom(g, j)
            sbk[0] += 1
            bi = 2 + sbk[0] % 2
            ptc[0] += 1
            pi = ptc[0] % 3
            st[n] = (bi, pi)
            lh, rh = maps[m]
            def mms(e):
                ins = e.matmul(pb[bi][:, q0:512], lhsT=lh(j), rhs=rh(g, q0), start=True, stop=(r < 0))
                if r >= 0:
                    ins = e.matmul(pb[bi][:, q0:q0 + 128], lhsT=identB[:], rhs=maskC[:], start=False, stop=True)
                return ins
            pe(mms, [t_KT[slot][j // 4], t_QT[slot][g], t_const], [t_pb[bi]])
            exp_emit(m, j, g, r, q0, pb[bi], PT[pi], [t_pb[bi]] + extra_r, [t_PT[pi]])

        def stage_pv(n):
            j, m = items[n]
            r, q0 = blk_geom(g, j)
            bi, pi = st[n]
            ob, db = obanks[m], dbanks[m]

            def mm(e):
                e.matmul(pb[ob][:, q0:512], lhsT=VV[slot][:, j, :], rhs=PT[pi][:, q0:512], start=(j == 0), stop=(j == nblk - 1))
                return e.matmul(pb[db][:, q0:512], lhsT=onesB[:], rhs=PT[pi][:, q0:512], start=(j == 0), stop=(j == nblk - 1))
            pe(mm, [t_V[slot][j // 4], t_PT[pi], t_const], [t_pb[ob], t_pb[db]])

        for n in range(len(items) + 1):
            if n < len(items):
                stage_s(n)
            if n >= 1:
                stage_pv(n - 1)
            yield

    def fox_head(b, slot, h, bs):
        for g in range(NG):
            yield from fox_group(b, slot, h, bs, g)

    def fox_group(b, slot, h, bs, g):
        ob, db = 4 + g % 2, 6 + g % 2
        if True:
            def exp_emit(m, j, g, r, q0, sbank, pt, rr, ww):
                def em(e):
                    ins = None
                    for u in range(max(r, 0), 4):
                        i = 4 * g + u
                        ins = e.activation(out=pt[:, u * 128:(u + 1) * 128], in_=sbank[:, u * 128:(u + 1) * 128], func=AF.Exp,
                                           bias=BIASF[bs][:, j, i:i + 1])
                    return ins
                act(em, rr, ww)
            yield from softmax_attn(b, slot, g, [(lambda j: KT[slot][:, j * 128:(j + 1) * 128], lambda g, q0: QT[slot][:, g * 512 + q0:(g + 1) * 512])],
                         exp_emit, [ob], [db], [t_BIASF[bs]])
            cs = slice(g * 512, (g + 1) * 512)
            act(lambda e: e.activation(out=TMP[0], in_=pb[db][:], func=AF.Ln), [t_pb[db]], [t_TMP[0]])
            act(lambda e: e.activation(out=TMP[0], in_=TMP[0], func=AF.Exp, scale=-1.0), [t_TMP[0]], [t_TMP[0]])
            dve(lambda e: e.tensor_tensor(out=TMP[1], in0=pb[ob][:], in1=TMP[0], op=ALU.mult), [t_pb[ob], t_TMP[0]], [t_TMP[1]])
            dve(lambda e, cs=cs: e.tensor_tensor(out=OT[:, b, cs], in0=TMP[1], in1=OT[:, b, cs], op=ALU.mult), [t_TMP[1], t_OT[b][g]], [t_OT[b][g]])

    def diff_head(b, slot, d):
        for g in range(NG):
            yield from diff_group(b, slot, d, g)

    def diff_group(b, slot, d, g):
        if True:
            def exp_emit(m, j, g, r, q0, sbank, pt, rr, ww):
                def em(e):
                    ins = None
                    for u2 in range(2):
                        lo = max(q0, 256 * u2)
                        hi = 256 * (u2 + 1)
                        if lo >= hi:
                            continue
                        gi = 2 * g + u2
                        ins = e.activation(out=pt[:, lo:hi], in_=sbank[:, lo:hi], func=AF.Exp, bias=biasD[:, d, j, gi:gi + 1])
                    return ins
                act(em, rr, ww)
            maps = [(lambda j, c=c: KT[slot][c * 64:(c + 1) * 64, j * 128:(j + 1) * 128],
                     lambda g, q0, c=c: QT[slot][c * 64:(c + 1) * 64, g * 512 + q0:(g + 1) * 512]) for c in range(2)]
            yield from softmax_attn(b, slot, g, maps, exp_emit, [4, 5], [6, 7], [t_const])
            cs = slice(g * 512, (g + 1) * 512)
            dve(lambda e: e.tensor_copy(out=TMP[1], in_=pb[4][:]), [t_pb[4]], [t_TMP[1]])
            act(lambda e: e.activation(out=TMP[0], in_=pb[6][:], func=AF.Ln), [t_pb[6]], [t_TMP[0]])
            dve(lambda e: e.tensor_copy(out=TMP[2], in_=pb[5][:]), [t_pb[5]], [t_TMP[2]])
            act(lambda e: e.activation(out=TMP[3], in_=pb[7][:], func=AF.Ln), [t_pb[7]], [t_TMP[3]])
            act(lambda e: e.activation(out=TMP[0], in_=TMP[0], func=AF.Exp, scale=-1.0), [t_TMP[0]], [t_TMP[0]])
            act(lambda e: e.activation(out=TMP[3], in_=TMP[3], func=AF.Exp, scale=-1.0), [t_TMP[3]], [t_TMP[3]])
            dve(lambda e: e.tensor_tensor(out=TMP[1], in0=TMP[1], in1=TMP[0], op=ALU.mult), [t_TMP[1], t_TMP[0]], [t_TMP[1]])
            dve(lambda e: e.tensor_tensor(out=TMP[2], in0=TMP[2], in1=TMP[3], op=ALU.mult), [t_TMP[2], t_TMP[3]], [t_TMP[2]])
            dve(lambda e: e.scalar_tensor_tensor(out=TMP[1], in0=TMP[2], scalar=lamt[:, 4:5], in1=TMP[1], op0=ALU.mult, op1=ALU.add),
                [t_TMP[2], t_TMP[1], t_lam], [t_TMP[1]])
            dve(lambda e: e.tensor_tensor(out=TMP[2], in0=TMP[1], in1=TMP[1], op=ALU.mult), [t_TMP[1]], [t_TMP[2]])
            pe(lambda e: e.matmul(pb[2][:], lhsT=meanF[:], rhs=TMP[2], start=True, stop=True), [t_TMP[2], t_const], [t_pb[2]])
            act(lambda e: e.activation(out=TMP[3], in_=pb[2][:], func=AF.Ln, bias=float(SUBLN_EPS)), [t_pb[2]], [t_TMP[3]])
            act(lambda e: e.activation(out=TMP[3], in_=TMP[3], func=AF.Exp, scale=-0.5), [t_TMP[3]], [t_TMP[3]])
            dve(lambda e: e.tensor_tensor(out=TMP[1], in0=TMP[1], in1=TMP[3], op=ALU.mult), [t_TMP[1], t_TMP[3]], [t_TMP[1]])
            dve(lambda e, cs=cs: e.scalar_tensor_tensor(out=OT[:, b, cs], in0=TMP[1], scalar=gcol[:, 0:1], in1=OT[:, b, cs],
                                                       op0=ALU.mult, op1=ALU.mult), [t_TMP[1], t_lam, t_OT[b][g]], [t_OT[b][g]])

    def sb_head(b, slot):
        for g in range(NG):
            yield from sb_group(b, slot, g)

    def sb_group(b, slot, g):
        ob = 4 + g % 2
        if True:
            nblk = 4 * g + 4
            js = list(range(nblk - 1, -1, -1))
            st = {}
            dve(lambda e: e.memset(ACC32, 0.0), (), [t_ACC32])
            pe(lambda e: e.matmul(pb[ob][:], lhsT=zerosB[:], rhs=QT[slot][:, g * 512:(g + 1) * 512], start=True, stop=False),
               [t_const, t_QT[slot][g]], [t_pb[ob]])

            def stage_z(n):
                j = js[n]
                r, q0 = blk_geom(g, j)
                zb = 2 + n % 2
                si = n % 2
                st[n] = (zb, si)
                kt = KT[slot][:, j * 128:(j + 1) * 128]
                qt = QT[slot][:, g * 512 + q0:(g + 1) * 512]
                def mmz(e):
                    ins = e.matmul(pb[zb][:, q0:512], lhsT=kt, rhs=qt, start=True, stop=(r < 0))
                    if r >= 0:
                        ins = e.matmul(pb[zb][:, q0:q0 + 128], lhsT=identB[:], rhs=maskS[:], start=False, stop=True)
                    return ins
                pe(mmz, [t_KT[slot][j // 4], t_QT[slot][g], t_const], [t_pb[zb]])
                act(lambda e: e.activation(out=pb[zb][:, q0:512], in_=pb[zb][:, q0:512], func=AF.Exp), [t_pb[zb]], [t_pb[zb]])
                act(lambda e: e.activation(out=SP_[si][:, q0:512], in_=pb[zb][:, q0:512], func=AF.Ln, bias=1.0), [t_pb[zb]], [t_SP[si]])

            def stage_arg(n):
                j = js[n]
                r, q0 = blk_geom(g, j)
                zb, si = st[n]
                ab = 6 + n % 2
                ptc[0] += 1
                pi = ptc[0] % 3
                st[n] = (zb, si, pi)
                kt = KT[slot][:, j * 128:(j + 1) * 128]
                qt = QT[slot][:, g * 512 + q0:(g + 1) * 512]
                rd = [t_KT[slot][j // 4], t_QT[slot][g], t_SP[si], t_const]
                if n > 0:
                    rd.append(t_ACCB[(n - 1) % 2])

                def mm(e):
                    e.matmul(pb[ab][:, q0:512], lhsT=kt, rhs=qt, start=True, stop=False)
                    if r >= 0:
                        e.matmul(pb[ab][:, q0:q0 + 128], lhsT=identB[:], rhs=maskS[:], start=False, stop=False)
                    ins = e.matmul(pb[ab][:, q0:512], lhsT=negTriB[:], rhs=SP_[si][:, q0:512], start=False, stop=(n == 0))
                    if n > 0:
                        ins = e.matmul(pb[ab][:, q0:512], lhsT=negOnesB[:], rhs=ACCB[(n - 1) % 2][:, q0:512], start=False, stop=True)
                    return ins
                pe(mm, rd, [t_pb[ab]])
                if n < nblk - 1:
                    dve(lambda e: e.tensor_tensor(out=ACC32[:, q0:512], in0=ACC32[:, q0:512], in1=SP_[si][:, q0:512], op=ALU.add),
                        [t_SP[si], t_ACC32], [t_ACC32])
                    dve(lambda e: e.tensor_copy(out=ACCB[n % 2], in_=ACC32), [t_ACC32], [t_ACCB[n % 2]])
                act(lambda e: e.activation(out=PT[pi][:, q0:512], in_=pb[ab][:, q0:512], func=AF.Exp), [t_pb[ab]], [t_PT[pi]])

            def stage_pv(n):
                j = js[n]
                r, q0 = blk_geom(g, j)
                zb, si, pi = st[n]
                last = (n == nblk - 1)
                pe(lambda e: e.matmul(pb[ob][:, q0:512], lhsT=VV[slot][:, j, :], rhs=PT[pi][:, q0:512], start=False, stop=last),
                   [t_V[slot][j // 4], t_PT[pi]], [t_pb[ob]])

            for n in range(nblk + 2):
                if n < nblk:
                    stage_z(n)
                if 0 <= n - 1 < nblk:
                    stage_arg(n - 1)
                if 0 <= n - 2 < nblk:
                    stage_pv(n - 2)
                yield
            cs = slice(g * 512, (g + 1) * 512)
            dve(lambda e, cs=cs: e.tensor_tensor(out=OT[:, b, cs], in0=pb[ob][:], in1=OT[:, b, cs], op=ALU.mult), [t_pb[ob], t_OT[b][g]], [t_OT[b][g]])

    all_qkv = [t for s_ in range(2) for g_ in range(NG) for t in (t_QT[s_][g_], t_KT[s_][g_], t_V[s_][g_])]
    all_misc = t_PT + t_SP + t_ACCB + [t_ACC32] + t_BIASF + t_TMP + t_GT

    def phase3_prologue(l):
        for cb in range(NCB):
            dma("pool", s_wo[cb], wo[cb], wout_d[l, cb], (), [t_wo[cb]] + t_xT)

    def phase3(l, src, dst):
        def emit(e):
            e.dma_start(out=lng_b, in_=lng_d[l:l + 1, :].partition_broadcast(128)).then_inc(s_lnp, 16)
            return e.dma_start(out=lnb_b, in_=lnb_d[l:l + 1, :].partition_broadcast(128)).then_inc(s_lnp, 16)
        sch.add("sp", emit, (), [t_lnp] + all_misc, dma_sem=s_lnp, ndma=2)
        FM = 512
        nch = (D + FM - 1) // FM
        while D % nch:
            nch += 1
        chw = D // nch
        def zload(t):
            zi = t % 3
            dma("sp", s_zin[zi], zsl[zi], src[t * 128:(t + 1) * 128, :], (), [t_z[zi]] + (all_qkv if t < 3 else []))
        for t in range(min(3, NT)):
            zload(t)
        for t in range(NT):
            zi = t % 3
            z = zsl[zi]
            yb = [(t % 2) * NCB + cb for cb in range(NCB)]

            def mm(e, t=t, yb=yb):
                ins = None
                for kc in range(KC):
                    for cb in range(NCB):
                        ins = e.matmul(pb[yb[cb]][:, 0:CBW], lhsT=OT[:, kc, t * 128:(t + 1) * 128], rhs=wo[cb][:, kc, :],
                                       start=(kc == 0), stop=(kc == KC - 1))
                return ins
            pe(mm, t_wo + [t_OT[b][t // 4] for b in range(NH)], [t_pb[i] for i in yb])
            for cb in range(NCB):
                dve(lambda e, cb=cb, z=z, yb=yb: e.scalar_tensor_tensor(out=z[:, cb * CBW:(cb + 1) * CBW], in0=z[:, cb * CBW:(cb + 1) * CBW],
                                                                      scalar=float(DEEPNORM_ALPHA), in1=pb[yb[cb]][:, 0:CBW],
                                                                      op0=ALU.mult, op1=ALU.add), [t_pb[yb[cb]], t_z[zi]], [t_z[zi]])
            for c in range(nch):
                dve(lambda e, c=c, z=z: e.bn_stats(out=bnst[:, c, :], in_=z[:, c * chw:(c + 1) * chw]), [t_z[zi], t_st], [t_st])
            dve(lambda e: e.bn_aggr(out=mv[:, 0:2], in_=bnst[:, 0:nch, :]), [t_st], [t_st])
            act(lambda e: e.activation(out=mv[:, 2:3], in_=mv[:, 1:2], func=AF.Ln, bias=float(LN_EPS)), [t_st], [t_st])
            act(lambda e: e.activation(out=mv[:, 3:4], in_=mv[:, 2:3], func=AF.Exp, scale=-0.5), [t_st], [t_st])
            dve(lambda e: e.scalar_tensor_tensor(out=mv[:, 4:5], in0=mv[:, 0:1], scalar=-1.0, in1=mv[:, 3:4], op0=ALU.mult, op1=ALU.mult),
                [t_st], [t_st])
            act(lambda e, z=z: e.activation(out=z, in_=z, func=AF.Identity, scale=mv[:, 3:4], bias=mv[:, 4:5]), [t_st, t_z[zi]], [t_z[zi], t_st])
            pool(lambda e, z=z: e.tensor_tensor(out=z, in0=z, in1=lng_b, op=ALU.mult), [t_z[zi], t_lnp], [t_z[zi]])
            pool(lambda e, z=z: e.tensor_tensor(out=z, in0=z, in1=lnb_b, op=ALU.add), [t_z[zi], t_lnp], [t_z[zi]])
            dma("sp", s_zout[zi], dst[t * 128:(t + 1) * 128, :], z, [t_z[zi]], ())
            if t + 3 < NT:
                zload(t + 3)

    stop = cfg.get("stop")
    consts()
    sch.barrier()
    for sq in range(NSEQ):
        for l in range(DEPTH):
            if stop == "consts":
                break
            src = x_d[sq] if l == 0 else scr_d[(l - 1) % 2]
            dst = out_d[sq] if l == DEPTH - 1 else scr_d[l % 2]
            phase1(src)
            if cfg.get("p1bar", False):
                sch.barrier()
            if stop == "p1":
                break
            load_params(l)
            if stop == "par":
                break
            def drain(gen):
                for _ in gen:
                    pass

            def interleave(main, side, n_main, n_side):
                n_main = max(1, int(n_main * 0.85))
                i = 0
                emitted = 0
                done = False
                for _ in main:
                    want = min(n_side, (i * n_side) // n_main + 1)
                    while not done and emitted < want:
                        try:
                            next(side)
                            emitted += 1
                        except StopIteration:
                            done = True
                    i += 1
                if not done:
                    drain(side)

            def attn_gen(b, slot):
                if b < NF:
                    yield from fox_head(b, slot, b, b % 2)
                elif b < NF + NS:
                    yield from sb_head(b, slot)
                else:
                    yield from diff_head(b, slot, b - NF - NS)

            def n_attn(b):
                base = sum(4 * g + 5 for g in range(NG))
                if b < NF:
                    return base
                if b < NF + NS:
                    return base + NG
                return 2 * base - NG

            def kind(b):
                return "f" if b < NF else ("s" if b < NF + NS else "d")

            if stop in ("inproj", "fbias"):
                for b in range(NH):
                    drain(inproj_head(l, b, 0, "f"))
                    if stop == "fbias":
                        fox_bias(b, b % 2)
                break
            prev = None
            for b in range(NH):
                ip = inproj_head(l, b, b % 2, kind(b))
                if prev is None:
                    def prep_gen():
                        if NF > 0:
                            yield from fox_prep(l)
                    interleave(ip, prep_gen(), 4 * NG, 4)
                    if NF > 0:
                        fox_bias(0, 0)
                else:
                    if b < NF:
                        fox_bias(b, b % 2)
                    interleave(prev, ip, n_attn(b - 1), 4 * NG)
                prev = attn_gen(b, b % 2)
            if stop == "attn":
                drain(prev)
                break

            def p3pro():
                phase3_prologue(l)
                yield
            interleave(prev, p3pro(), 1, 1)
            phase3(l, src, dst)
            sch.barrier()
    sch.finish()
    sch.emit_all(nc, es)
    es.close()
    return nc


def layout_weights(cfg, w_in, w_out):
    NF, NS, ND = cfg["NF"], cfg["NS"], cfg["ND"]
    NH = NF + NS + ND
    D = 128 * NH
    KC = NH
    DEPTH = w_in.shape[0]
    FW, SW, DW = NF * 128, NS * 128, ND * 128
    offs = []
    for h in range(NF):
        offs.append([k * FW + h * 128 for k in range(4)])
    base = 4 * FW
    for h in range(NS):
        offs.append([base + k * SW + h * 128 for k in range(4)])
    base = 4 * FW + 4 * SW
    for h in range(ND):
        offs.append([base + k * DW + h * 128 for k in range(4)])
    ffo = 4 * FW + 4 * SW + 4 * DW
    w_in_r = np.empty((DEPTH, NH, 4, 128, KC, 128), np.float32)
    for b in range(NH):
        for c in range(4):
            blk = w_in[:, :, offs[b][c]:offs[b][c] + 128]
            w_in_r[:, b, c] = blk.reshape(DEPTH, KC, 128, 128).transpose(0, 2, 1, 3)
    NFp = max(NF, 1)
    w_ff_r = np.zeros((DEPTH, 128, KC, NFp), np.float32)
    if NF > 0:
        w_ff_r[:] = w_in[:, :, ffo:ffo + NF].reshape(DEPTH, KC, 128, NF).transpose(0, 2, 1, 3)
    CBW = min(512, D)
    NCB = D // CBW
    w_out_r = np.ascontiguousarray(
        w_out.reshape(DEPTH, KC, 128, NCB, CBW).transpose(0, 3, 2, 1, 4))
    return w_in_r, w_ff_r, w_out_r


def run(cfg, x, w_in, b_f, diff_lambda, diff_subln_g, w_out, ln_g, ln_b, n_cores):
    NSEQ = cfg["NSEQ"]
    DEPTH = cfg["DEPTH"]
    x = np.ascontiguousarray(np.asarray(x, np.float32))
    w_in_r, w_ff_r, w_out_r = layout_weights(cfg, np.asarray(w_in, np.float32), np.asarray(w_out, np.float32))
    NFp = max(cfg["NF"], 1)
    bfp = np.zeros((DEPTH, NFp), np.float32)
    if cfg["NF"] > 0:
        bfp[:] = np.asarray(b_f, np.float32)
    common = {
        "w_in_r": w_in_r, "w_ff_r": w_ff_r, "w_out_r": w_out_r, "b_f": bfp,
        "diff_lambda": np.ascontiguousarray(np.asarray(diff_lambda, np.float32).reshape(DEPTH, 256)),
        "diff_subln_g": np.ascontiguousarray(np.asarray(diff_subln_g, np.float32)),
        "ln_g": np.ascontiguousarray(np.asarray(ln_g, np.float32)),
        "ln_b": np.ascontiguousarray(np.asarray(ln_b, np.float32)),
    }
    nc = build_nc(cfg)
    in_maps = []
    for c in range(n_cores):
        m = dict(common)
        m["x"] = np.ascontiguousarray(x[c * NSEQ:(c + 1) * NSEQ])
        in_maps.append(m)
    res = run_bass_kernel_spmd(nc, in_maps, core_ids=list(range(n_cores)))
    return np.concatenate([np.asarray(r["out"], np.float32) for r in res.results], axis=0)


def kernel(x, w_in, b_f, diff_lambda, diff_subln_g, w_out, ln_g, ln_b):
    return run(FULL_CFG, x, w_in, b_f, diff_lambda, diff_subln_g, w_out, ln_g, ln_b, 8)
```

```python
import math
from contextlib import ExitStack

import numpy as np
import concourse.bass as bass
import concourse.mybir as mybir
from concourse.bass_utils import run_bass_kernel_spmd

F32 = mybir.dt.float32
BF16 = mybir.dt.bfloat16
AF = mybir.ActivationFunctionType
ALU = mybir.AluOpType
AX = mybir.AxisListType

FULL_CFG = dict(S=2048, NF=6, NS=6, ND=4, DEPTH=4, NSEQ=2)
DEEPNORM_ALPHA = (2 * 4) ** 0.25
LN_EPS = 1e-5
SUBLN_EPS = 1e-5
STREAMS = ("pe", "act", "dve", "pool", "sp")
MAXC = 20000
SAME_ENGINE_RAW = True


class Tile:
    __slots__ = ("name", "w", "r")

    def __init__(self, name):
        self.name = name
        self.w = None
        self.r = []


class Op:
    __slots__ = ("stream", "idx", "emit", "deps", "dma_sem", "dma_val", "signal", "signo", "raw_same")

    def __init__(self, stream, idx, emit):
        self.stream = stream
        self.idx = idx
        self.emit = emit
        self.deps = []
        self.dma_sem = None
        self.dma_val = 0
        self.signal = False
        self.signo = 0
        self.raw_same = False


class Sched:
    def __init__(self):
        self.streams = {s: [] for s in STREAMS}
        self.pending = {s: [] for s in STREAMS}
        self.dma_since_barrier = []
        self.dma_counts = {}

    def add(self, stream, emit, reads=(), writes=(), dma_sem=None, ndma=1):
        op = Op(stream, len(self.streams[stream]), emit)
        deps = {}

        def dep(o, raw):
            if o is None:
                return
            k = id(o)
            if k in deps:
                deps[k] = (o, deps[k][1] or raw)
            else:
                deps[k] = (o, raw)

        for t in reads:
            dep(t.w, True)
        for t in writes:
            dep(t.w, False)
            for o in t.r:
                dep(o, False)
        for o in self.pending[stream]:
            dep(o, True)
        self.pending[stream] = []
        best = {}
        out = []
        for o, raw in deps.values():
            if o.dma_sem is not None:
                out.append((o, raw))
            else:
                b = best.get(o.stream)
                if b is None or o.idx > b[0].idx:
                    best[o.stream] = (o, raw)
                elif o.idx == b[0].idx and raw:
                    best[o.stream] = (o, True)
        for s, (o, raw) in best.items():
            if s == stream and o.dma_sem is None:
                if stream == "pe" or stream == "sp":
                    continue
            out.append((o, raw))
        op.deps = [o for o, _ in out]
        for o in op.deps:
            o.signal = True
        if dma_sem is not None:
            op.dma_sem = dma_sem
            c = self.dma_counts.get(id(dma_sem), 0) + ndma
            self.dma_counts[id(dma_sem)] = c
            op.dma_val = 16 * c
            self.dma_since_barrier.append(op)
        for t in reads:
            t.r.append(op)
            if len(t.r) > 24:
                keep = {}
                dm = []
                for o in t.r:
                    if o.dma_sem is not None:
                        dm.append(o)
                    else:
                        k = keep.get(o.stream)
                        if k is None or o.idx > k.idx:
                            keep[o.stream] = o
                t.r = list(keep.values()) + dm[-16:]
                if len(dm) > 16:
                    t.r = list(keep.values()) + dm
        for t in writes:
            t.w = op
            t.r = []
        self.streams[stream].append(op)
        return op

    def barrier(self):
        lasts = [self.streams[s][-1] for s in STREAMS if self.streams[s]]
        for s in STREAMS:
            self.pending[s] = self.pending[s] + list(lasts) + list(self.dma_since_barrier)
        self.dma_since_barrier = []

    def finish(self):
        self.barrier()
        self.add("sp", None)

    def emit_all(self, nc, es):
        nsig = {}
        for s in STREAMS:
            n = 0
            for op in self.streams[s]:
                if op.signal and op.dma_sem is None:
                    n += 1
                    op.signo = n
            nsig[s] = n
        sems = {}
        for s in STREAMS:
            nb = max(1, (nsig[s] + MAXC - 1) // MAXC)
            sems[s] = [es.enter_context(nc.semaphore(f"sem_{s}_{i}")) for i in range(nb)]
        block = es.enter_context(nc.Block())

        def run(stream, eng):
            waited = {s: 0 for s in STREAMS}
            dwaited = {}
            for op in self.streams[stream]:
                for d in op.deps:
                    if d.dma_sem is not None:
                        k = id(d.dma_sem)
                        if dwaited.get(k, 0) >= d.dma_val:
                            continue
                        eng.wait_ge(d.dma_sem, d.dma_val)
                        dwaited[k] = d.dma_val
                    else:
                        if waited[d.stream] >= d.signo:
                            continue
                        n = d.signo - 1
                        eng.wait_ge(sems[d.stream][n // MAXC], (n % MAXC) + 1)
                        waited[d.stream] = d.signo
                if op.emit is None:
                    continue
                ins = op.emit(eng)
                if op.dma_sem is not None:
                    pass
                elif op.signal:
                    n = op.signo - 1
                    ins.then_inc(sems[stream][n // MAXC], 1)

        @block.tensor
        def _(e):
            run("pe", e)

        @block.scalar
        def _(e):
            run("act", e)

        @block.vector
        def _(e):
            run("dve", e)

        @block.gpsimd
        def _(e):
            run("pool", e)

        @block.sync
        def _(e):
            run("sp", e)


def build_nc(cfg):
    S, NF, NS, ND, DEPTH, NSEQ = (cfg[k] for k in ("S", "NF", "NS", "ND", "DEPTH", "NSEQ"))
    NH = NF + NS + ND
    D = 128 * NH
    KC = NH
    NT = S // 128
    NG = S // 512
    CBW = min(512, D)
    NCB = D // CBW
    assert NCB * CBW == D and NCB <= 4
    NWS = 5
    NFp = max(NF, 1)

    nc = bass.Bass("TRN2", target_bir_lowering=False)
    x_d = nc.dram_tensor("x", [NSEQ, S, D], F32, kind="ExternalInput").ap()
    win_d = nc.dram_tensor("w_in_r", [DEPTH, NH, 4, 128, KC, 128], F32, kind="ExternalInput").ap()
    wff_d = nc.dram_tensor("w_ff_r", [DEPTH, 128, KC, NFp], F32, kind="ExternalInput").ap()
    wout_d = nc.dram_tensor("w_out_r", [DEPTH, NCB, 128, KC, CBW], F32, kind="ExternalInput").ap()
    bf_d = nc.dram_tensor("b_f", [DEPTH, NFp], F32, kind="ExternalInput").ap()
    dl_d = nc.dram_tensor("diff_lambda", [DEPTH, 256], F32, kind="ExternalInput").ap()
    sg_d = nc.dram_tensor("diff_subln_g", [DEPTH, 128], F32, kind="ExternalInput").ap()
    lng_d = nc.dram_tensor("ln_g", [DEPTH, D], F32, kind="ExternalInput").ap()
    lnb_d = nc.dram_tensor("ln_b", [DEPTH, D], F32, kind="ExternalInput").ap()
    out_d = nc.dram_tensor("out", [NSEQ, S, D], F32, kind="ExternalOutput").ap()
    scr_d = nc.dram_tensor("scr", [2, S, D], F32, kind="Internal").ap()

    sch = Sched()
    es = ExitStack()
    E = es.enter_context

    def sb(name, shape, dt):
        return E(nc.sbuf_tensor(name, shape, dt))

    def dsem(name):
        return E(nc.semaphore(name))

    RSZ = max(KC * S, NCB * KC * CBW)
    R = sb("R", [128, RSZ], BF16)
    xT = R[:, 0:KC * S].rearrange("p (k s) -> p k s", k=KC)
    wo = [R[:, cb * KC * CBW:(cb + 1) * KC * CBW].rearrange("p (k c) -> p k c", k=KC) for cb in range(NCB)]
    OTSZ = max(NH * S, 2 * 2 * D + 2 * D)
    OTr = sb("OT", [128, OTSZ], BF16)
    OT = OTr[:, 0:NH * S].rearrange("p (h s) -> p h s", h=NH)
    NXB = 4
    xb = [OTr[:, i * D:(i + 1) * D] for i in range(NXB)]
    QKVSZ = max(2 * 3 * S, 3 * 2 * D)
    QKVr = sb("QKV", [128, QKVSZ], BF16)
    QT = [QKVr[:, (3 * s + 0) * S:(3 * s + 1) * S] for s in range(2)]
    KT = [QKVr[:, (3 * s + 1) * S:(3 * s + 2) * S] for s in range(2)]
    VV = [QKVr[:, (3 * s + 2) * S:(3 * s + 3) * S].rearrange("p (t d) -> p t d", d=128) for s in range(2)]
    zsl = [QKVr[:, i * 2 * D:(i + 1) * 2 * D].bitcast(F32) for i in range(3)]
    Wsl = [sb(f"W{i}", [128, KC, 128], BF16) for i in range(NWS)]
    wff = sb("wff", [128, KC, NFp], BF16)
    MISC_P2 = 3 * 512 + 2 * 512 + 2 * 512 + 2 * 512
    MISC_F32 = 2 * NT * NT + 6 * 512
    MSZ = max(MISC_P2 + 2 * MISC_F32, 2 * 2 * D)
    Mr = sb("MISC", [128, MSZ], BF16)
    off = 0

    def carve(n, dt=BF16):
        nonlocal off
        if dt == F32:
            a = Mr[:, off:off + 2 * n].bitcast(F32)
            off += 2 * n
        else:
            a = Mr[:, off:off + n]
            off += n
        return a

    PT = [carve(512) for _ in range(3)]
    SP_ = [carve(512) for _ in range(2)]
    ACCB = [carve(512) for _ in range(2)]
    ACC32 = carve(512, F32)
    BIASF = [carve(NT * NT, F32).rearrange("p (j i) -> p j i", j=NT) for _ in range(2)]
    TMP = [carve(512, F32) for _ in range(4)]
    GT = [carve(512, F32) for _ in range(2)]
    assert off <= MSZ
    lng_b = Mr[:, 0:2 * D].bitcast(F32)
    lnb_b = Mr[:, 2 * D:4 * D].bitcast(F32)

    identB = sb("identB", [128, 128], BF16)
    onesB = sb("onesB", [128, 128], BF16)
    zerosB = sb("zerosB", [128, 128], BF16)
    negOnesB = sb("negOnesB", [128, 128], BF16)
    negTriB = sb("negTriB", [128, 128], BF16)
    maskC = sb("maskC", [128, 128], BF16)
    maskS = sb("maskS", [128, 128], BF16)
    negF = TMP[0][:, 0:128]
    cF = TMP[1][:, 0:128]
    bigF = TMP[2][:, 0:128]
    onesF = sb("onesF", [128, 128], F32)
    triuF = sb("triuF", [128, 128], F32)
    sel64F = sb("sel64F", [128, 128], F32)
    meanF = sb("meanF", [128, 128], F32)
    tokF = sb("tokF", [128, NT], F32)
    NGI = 2 * NG
    biasD = sb("biasD", [128, max(ND, 1), NT, NGI], F32)
    bfb = sb("bfb", [128, NFp], F32)
    dlb = sb("dlb", [128, 256], F32)
    sgc = sb("sgc", [128, 1], F32)
    gcol = sb("gcol", [128, 1], F32)
    lamt = sb("lamt", [128, 8], F32)
    dtmp = sb("dtmp", [128, 64], F32)
    nl = sb("nl", [128, NT, NFp], F32)
    Tsb = sb("Tsb", [128, NT, NFp], F32)
    pre = sb("pre", [128, NT, NFp], F32)
    ncol = sb("ncol", [128, NT, NFp], F32)
    midsb = sb("midsb", [128, NT, NFp], F32)
    bnst = sb("bnst", [128, 8, 6], F32)
    mv = sb("mv", [128, 8], F32)

    pb = [E(nc.psum_tensor(f"pb{i}", [128, 512], F32)) for i in range(8)]

    T = Tile
    t_xT = [T(f"xT{t}") for t in range(NT)]
    t_wo = [T(f"wo{c}") for c in range(NCB)]
    t_OT = [[T(f"OT{b}_{g}") for g in range(NG)] for b in range(NH)]
    t_xb = [T(f"xb{i}") for i in range(NXB)]
    t_QT = [[T(f"QT{s}_{g}") for g in range(NG)] for s in range(2)]
    t_KT = [[T(f"KT{s}_{g}") for g in range(NG)] for s in range(2)]
    t_V = [[T(f"V{s}_{g}") for g in range(NG)] for s in range(2)]
    t_z = [T(f"z{i}") for i in range(3)]
    t_W = [T(f"W{i}") for i in range(NWS)]
    t_wff = T("wff")
    t_PT = [T(f"PT{i}") for i in range(3)]
    t_SP = [T(f"SP{i}") for i in range(2)]
    t_ACCB = [T(f"ACCB{i}") for i in range(2)]
    t_ACC32 = T("ACC32")
    t_BIASF = [T("BIASF0"), T("BIASF1")]
    t_TMP = [T(f"TMP{i}") for i in range(4)]
    t_GT = [T("GT0"), T("GT1")]
    t_lnp = T("lnp")
    t_const = T("const")
    t_par = T("par")
    t_lam = T("lam")
    t_fox = T("foxprep")
    t_st = T("stats")
    t_pb = [T(f"pb{i}") for i in range(8)]
    t_dram = T("dram")

    s_xb = [dsem(f"s_xb{i}") for i in range(NXB)]
    s_zin = [dsem(f"s_zin{i}") for i in range(3)]
    s_zout = [dsem(f"s_zout{i}") for i in range(3)]
    s_W = [dsem(f"s_W{i}") for i in range(NWS)]
    s_wff = dsem("s_wff")
    s_wo = [dsem(f"s_wo{i}") for i in range(NCB)]
    s_lnp = dsem("s_lnp")
    s_par = dsem("s_par")

    pe = lambda emit, r=(), w=(): sch.add("pe", emit, r, w)
    act = lambda emit, r=(), w=(): sch.add("act", emit, r, w)
    dve = lambda emit, r=(), w=(): sch.add("dve", emit, r, w)
    pool = lambda emit, r=(), w=(): sch.add("pool", emit, r, w)

    pool_dmas = []
    MAX_SWDGE_INFLIGHT = 3

    def dma(stream, sem, out, in_, r=(), w=()):
        def emit(e):
            return e.dma_start(out=out, in_=in_).then_inc(sem, 16)
        if stream == "pool":
            if len(pool_dmas) >= MAX_SWDGE_INFLIGHT:
                sch.pending["pool"] = sch.pending["pool"] + [pool_dmas[-MAX_SWDGE_INFLIGHT]]
        op = sch.add(stream, emit, r, w, dma_sem=sem)
        if stream == "pool":
            pool_dmas.append(op)
        return op

    def consts():
        pool(lambda e: e.memset(onesF[:], 1.0), (), [t_const])
        pool(lambda e: e.memset(negF, -1.0), (), [t_const])
        pool(lambda e: e.memset(meanF[:], 1.0 / 128.0), (), [t_const])
        pool(lambda e: e.memset(zerosB[:], 0.0), (), [t_const])
        pool(lambda e: e.memset(onesB[:], 1.0), (), [t_const])
        pool(lambda e: e.memset(negOnesB[:], -1.0), (), [t_const])

        def sel(out, in_, pat, cm, base, op):
            pool(lambda e: e.affine_select(out=out, in_=in_, pattern=pat, compare_op=op, fill=0.0,
                                           base=base, channel_multiplier=cm), [t_const], [t_const])

        sel(triuF[:], onesF[:], [[1, 128]], -1, 0, ALU.is_ge)
        sel(sel64F[:], onesF[:], [[0, 128]], 1, -64, ALU.is_equal)
        sel(cF, onesF[:], [[1, 128]], -1, 0, ALU.is_equal)
        pool(lambda e: e.tensor_copy(out=identB[:], in_=cF), [t_const], [t_const])
        pool(lambda e: e.memset(bigF, -30000.0), (), [t_const])
        sel(cF, bigF, [[-1, 128]], 1, -1, ALU.is_ge)
        pool(lambda e: e.tensor_copy(out=maskC[:], in_=cF), [t_const], [t_const])
        sel(cF, bigF, [[-1, 128]], 1, 0, ALU.is_ge)
        pool(lambda e: e.tensor_copy(out=maskS[:], in_=cF), [t_const], [t_const])
        sel(cF, negF, [[-1, 128]], 1, 0, ALU.is_ge)
        pool(lambda e: e.tensor_copy(out=negTriB[:], in_=cF), [t_const], [t_const])
        pool(lambda e: e.iota(tokF[:], pattern=[[128, NT]], base=0, channel_multiplier=1,
                              allow_small_or_imprecise_dtypes=True), (), [t_const])
        for d in range(ND):
            slope = 2.0 ** (-8.0 * (d + 1) / ND)
            for gi in range(NGI):
                cen = 256 * gi + 128
                pool(lambda e, d=d, gi=gi, cen=cen, slope=slope: e.tensor_scalar(
                    out=biasD[:, d, :, gi], in0=tokF[:], scalar1=float(-cen), scalar2=float(slope),
                    op0=ALU.add, op1=ALU.mult), [t_const], [t_const])

    evac_rr = [0]

    def evac_copy(out, in_, r, w):
        evac_rr[0] += 1
        if evac_rr[0] % 2:
            dve(lambda e: e.tensor_copy(out=out, in_=in_), r, w)
        else:
            act(lambda e: e.activation(out=out, in_=in_, func=AF.Copy), r, w)

    def phase1(src):
        for t in range(NT):
            sl = t % NXB
            dma("pool", s_xb[sl], xb[sl], src[t * 128:(t + 1) * 128, :], (), [t_xb[sl]])
            for c0 in range(0, KC, 8):
                n = min(8, KC - c0)
                bi = ((t * ((KC + 7) // 8)) + c0 // 8) % 2
                bank = pb[bi][:].bitcast(BF16)

                def tr(e, sl=sl, c0=c0, n=n, bank=bank):
                    ins = None
                    for i in range(n):
                        ins = e.transpose(bank[:, i * 128:(i + 1) * 128], xb[sl][:, (c0 + i) * 128:(c0 + i + 1) * 128], identB[:])
                    return ins
                pe(tr, [t_xb[sl], t_const], [t_pb[bi]])
                evac_copy(xT[:, c0:c0 + n, t * 128:(t + 1) * 128],
                          bank[:, 0:n * 128].rearrange("p (c s) -> p c s", c=n), [t_pb[bi]], [t_xT[t]])

    wlist = [(l, b, c) for _sq in range(NSEQ) for l in range(DEPTH) for b in range(NH) for c in (0, 1, 3, 2)]
    wissued = [0]
    wuse = [0]
    WAHEAD = 2

    def w_prefetch(upto):
        while wissued[0] <= min(upto, len(wlist) - 1):
            n = wissued[0]
            l, b, c = wlist[n]
            i = n % NWS
            dma("pool", s_W[i], Wsl[i][:], win_d[l, b, c], (), [t_W[i]])
            wissued[0] += 1

    def next_w(l, b, c):
        n = wuse[0]
        assert wlist[n] == (l, b, c)
        wuse[0] += 1
        w_prefetch(n + WAHEAD)
        return n % NWS

    ipb = [0]
    gctr = [0]

    def inproj_bank():
        ipb[0] += 1
        return ipb[0] % 2

    def inproj_head(l, b, slot, kind):
        qscale = (64 if kind == "d" else 128) ** -0.5
        wi = {}
        pend = []

        def flush(n):
            for _ in range(n):
                if pend:
                    pend.pop(0)()

        for c in (0, 1, 3):
            wi[c] = next_w(l, b, c)
            for tg in range(NG):
                flush(2)
                bi = inproj_bank()

                def mm(e, bi=bi, w=wi[c], tg=tg, k0=0, k1=KC):
                    ins = None
                    for kc in range(k0, k1):
                        ins = e.matmul(pb[bi][:], lhsT=Wsl[w][:, kc, :], rhs=xT[:, kc, tg * 512:(tg + 1) * 512],
                                       start=(kc == 0), stop=(kc == KC - 1))
                    return ins
                kh = KC // 2
                if kh >= 1:
                    pe(lambda e, mm=mm, kh=kh: mm(e, k0=0, k1=kh), [t_W[wi[c]]] + t_xT[4 * tg:4 * tg + 4], [t_pb[bi]])
                    yield
                    pe(lambda e, mm=mm, kh=kh: mm(e, k0=kh, k1=KC), [t_W[wi[c]]] + t_xT[4 * tg:4 * tg + 4], [t_pb[bi]])
                else:
                    pe(mm, [t_W[wi[c]]] + t_xT[4 * tg:4 * tg + 4], [t_pb[bi]])
                cs = slice(tg * 512, (tg + 1) * 512)
                if c == 0:
                    dve(lambda e, bi=bi, cs=cs: e.tensor_scalar(out=QT[slot][:, cs], in0=pb[bi][:], scalar1=float(qscale),
                                                                scalar2=None, op0=ALU.mult), [t_pb[bi]], [t_QT[slot][tg]])
                elif c == 1:
                    dve(lambda e, bi=bi, cs=cs: e.tensor_copy(out=KT[slot][:, cs], in_=pb[bi][:]), [t_pb[bi]], [t_KT[slot][tg]])
                else:
                    gi_ = gctr[0] % 2
                    gctr[0] += 1
                    gt = GT[gi_]
                    tgt = t_GT[gi_]
                    ot_w = [t_OT[b][tg]] + (t_xb if b * S < NXB * D else [])
                    dve(lambda e, bi=bi, cs=cs: e.tensor_copy(out=OT[:, b, cs], in_=pb[bi][:]), [t_pb[bi]], ot_w)
                    act(lambda e, cs=cs, gt=gt: e.activation(out=gt, in_=OT[:, b, cs], func=AF.Exp, scale=-1.0), [t_OT[b][tg]], [tgt])
                    pend.append(lambda gt=gt, tgt=tgt: act(lambda e: e.activation(out=gt, in_=gt, func=AF.Ln, bias=1.0), [tgt], [tgt]))
                    pend.append(lambda gt=gt, tgt=tgt: act(lambda e: e.activation(out=gt, in_=gt, func=AF.Exp, scale=-1.0), [tgt], [tgt]))
                    pend.append(lambda gt=gt, tgt=tgt, cs=cs, tg=tg: pool(
                        lambda e: e.tensor_tensor(out=OT[:, b, cs], in0=OT[:, b, cs], in1=gt, op=ALU.mult), [tgt, t_OT[b][tg]], [t_OT[b][tg]]))
                yield
        wi[2] = next_w(l, b, 2)
        for tg in range(NG):
            flush(2)
            bi = inproj_bank()

            def mmv(e, u, bi=bi, w=wi[2], tg=tg):
                ins = None
                t = 4 * tg + u
                for kc in range(KC):
                    ins = e.matmul(pb[bi][:, u * 128:(u + 1) * 128], lhsT=xT[:, kc, t * 128:(t + 1) * 128],
                                   rhs=Wsl[w][:, kc, :], start=(kc == 0), stop=(kc == KC - 1))
                return ins
            for u in range(4):
                pe(lambda e, mmv=mmv, u=u: mmv(e, u), [t_W[wi[2]], t_xT[4 * tg + u]], [t_pb[bi]])
                if u < 3:
                    yield
            dve(lambda e, bi=bi, tg=tg: e.tensor_copy(out=VV[slot][:, 4 * tg:4 * tg + 4, :],
                                                      in_=pb[bi][:].rearrange("p (u d) -> p u d", u=4)),
                [t_pb[bi]], [t_V[slot][tg]])
            yield
        while pend:
            flush(2)
            yield

    def load_params(l):
        lam_init = 0.8 - 0.6 * math.exp(-0.3 * l)

        def emit(e):
            e.dma_start(out=bfb[:], in_=bf_d[l:l + 1, :].partition_broadcast(128)).then_inc(s_par, 16)
            e.dma_start(out=dlb[:], in_=dl_d[l:l + 1, :].partition_broadcast(128)).then_inc(s_par, 16)
            return e.dma_start(out=sgc[:], in_=sg_d[l:l + 1, :].rearrange("o p -> p o")).then_inc(s_par, 16)
        sch.add("sp", emit, (), [t_par], dma_sem=s_par, ndma=3)
        if ND > 0:
            dve(lambda e: e.tensor_tensor(out=dtmp[:], in0=dlb[:, 0:64], in1=dlb[:, 64:128], op=ALU.mult), [t_par], [t_lam])
            dve(lambda e: e.reduce_sum(out=lamt[:, 0:1], in_=dtmp[:], axis=AX.X), [t_lam], [t_lam])
            dve(lambda e: e.tensor_tensor(out=dtmp[:], in0=dlb[:, 128:192], in1=dlb[:, 192:256], op=ALU.mult), [t_par, t_lam], [t_lam])
            dve(lambda e: e.reduce_sum(out=lamt[:, 1:2], in_=dtmp[:], axis=AX.X), [t_lam], [t_lam])
            act(lambda e: e.activation(out=lamt[:, 2:4], in_=lamt[:, 0:2], func=AF.Exp), [t_lam], [t_lam])
            dve(lambda e: e.scalar_tensor_tensor(out=lamt[:, 4:5], in0=lamt[:, 3:4], scalar=float(-lam_init), in1=lamt[:, 2:3],
                                                 op0=ALU.add, op1=ALU.subtract), [t_lam], [t_lam])
            dve(lambda e: e.tensor_scalar(out=gcol[:], in0=sgc[:], scalar1=float(1.0 - lam_init), scalar2=None, op0=ALU.mult),
                [t_par, t_lam], [t_lam])

    def fox_prep(l):
        NC_ = NT * NF
        dma("pool", s_wff, wff[:], wff_d[l], (), [t_wff])

        def mmf(e):
            ins = None
            for t in range(NT):
                for kc in range(KC):
                    ins = e.matmul(pb[5][:, t * NF:(t + 1) * NF], lhsT=xT[:, kc, t * 128:(t + 1) * 128], rhs=wff[:, kc, 0:NF],
                                   start=(kc == 0), stop=(kc == KC - 1))
            return ins
        pe(mmf, [t_wff] + t_xT, [t_pb[5]])
        yield
        for t in range(NT):
            dve(lambda e, t=t: e.tensor_tensor(out=nl[:, t, 0:NF], in0=pb[5][:, t * NF:(t + 1) * NF], in1=bfb[:, 0:NF], op=ALU.add),
                [t_pb[5], t_par], [t_fox])
        nl2 = nl[:].rearrange("p t f -> p (t f)")
        act(lambda e: e.activation(out=nl2, in_=nl2, func=AF.Exp, scale=-1.0), [t_fox], [t_fox])
        act(lambda e: e.activation(out=nl2, in_=nl2, func=AF.Ln, bias=1.0), [t_fox], [t_fox])
        yield
        pe(lambda e: e.matmul(pb[5][:, 0:NC_], lhsT=onesF[:], rhs=nl2, start=True, stop=True), [t_fox, t_const], [t_pb[5]])
        pe(lambda e: e.matmul(pb[5][:, 128:128 + NC_], lhsT=triuF[:], rhs=nl2, start=True, stop=True), [t_fox, t_const], [t_pb[5]])
        dve(lambda e: e.tensor_copy(out=Tsb[:].rearrange("p t f -> p (t f)"), in_=pb[5][:, 0:NC_]), [t_pb[5]], [t_fox])
        yield
        dve(lambda e: e.memset(pre[:, 0, :], 0.0), [t_fox], [t_fox])
        for j in range(1, NT):
            dve(lambda e, j=j: e.tensor_tensor(out=pre[:, j, :], in0=pre[:, j - 1, :], in1=Tsb[:, j - 1, :], op=ALU.add), [t_fox], [t_fox])
        dve(lambda e: e.tensor_tensor(out=ncol[:].rearrange("p t f -> p (t f)"), in0=pb[5][:, 128:128 + NC_],
                                      in1=pre[:].rearrange("p t f -> p (t f)"), op=ALU.add), [t_pb[5], t_fox], [t_fox])
        pe(lambda e: e.matmul(pb[5][:, 256:256 + NC_], lhsT=sel64F[:], rhs=ncol[:].rearrange("p t f -> p (t f)"), start=True, stop=True),
           [t_fox, t_const], [t_pb[5]])
        dve(lambda e: e.tensor_copy(out=midsb[:].rearrange("p t f -> p (t f)"), in_=pb[5][:, 256:256 + NC_]), [t_pb[5]], [t_fox])

    def fox_bias(h, bs):
        def em(e):
            ins = None
            for i in range(NT):
                ins = e.tensor_scalar(out=BIASF[bs][:, :, i], in0=ncol[:, :, h], scalar1=midsb[:, i, h:h + 1], scalar2=None,
                                      op0=ALU.subtract)
            return ins
        dve(em, [t_fox], [t_BIASF[bs]])

    ptc = [0]
    sbk = [0]

    def blk_geom(g, j):
        r = j - 4 * g
        q0 = 128 * r if r >= 0 else 0
        return r, q0

    def softmax_attn(b, slot, g, maps, exp_emit, obanks, dbanks, extra_r):
        nblk = 4 * g + 4
        items = [(j, m) for j in range(nblk) for m in range(len(maps))]
        st = {}

        def stage_s(n):
            j, m = items[n]
            r, q0 = blk_geom(g, j)
            sbk[0] += 1
            bi = 2 + sbk[0] % 2
            ptc[0] += 1
            pi = ptc[0] % 3
            st[n] = (bi, pi)
            lh, rh = maps[m]
            def mms(e):
                ins = e.matmul(pb[bi][:, q0:512], lhsT=lh(j), rhs=rh(g, q0), start=True, stop=(r < 0))
                if r >= 0:
                    ins = e.matmul(pb[bi][:, q0:q0 + 128], lhsT=identB[:], rhs=maskC[:], start=False, stop=True)
                return ins
            pe(mms, [t_KT[slot][j // 4], t_QT[slot][g], t_const], [t_pb[bi]])
            exp_emit(m, j, g, r, q0, pb[bi], PT[pi], [t_pb[bi]] + extra_r, [t_PT[pi]])

        def stage_pv(n):
            j, m = items[n]
            r, q0 = blk_geom(g, j)
            bi, pi = st[n]
            ob, db = obanks[m], dbanks[m]

            def mm(e):
                e.matmul(pb[ob][:, q0:512], lhsT=VV[slot][:, j, :], rhs=PT[pi][:, q0:512], start=(j == 0), stop=(j == nblk - 1))
                return e.matmul(pb[db][:, q0:512], lhsT=onesB[:], rhs=PT[pi][:, q0:512], start=(j == 0), stop=(j == nblk - 1))
            pe(mm, [t_V[slot][j // 4], t_PT[pi], t_const], [t_pb[ob], t_pb[db]])

        for n in range(len(items) + 1):
            if n < len(items):
                stage_s(n)
            if n >= 1:
                stage_pv(n - 1)
            yield

    def fox_head(b, slot, h, bs):
        for g in range(NG):
            yield from fox_group(b, slot, h, bs, g)

    def fox_group(b, slot, h, bs, g):
        ob, db = 4 + g % 2, 6 + g % 2
        if True:
            def exp_emit(m, j, g, r, q0, sbank, pt, rr, ww):
                def em(e):
                    ins = None
                    for u in range(max(r, 0), 4):
                        i = 4 * g + u
                        ins = e.activation(out=pt[:, u * 128:(u + 1) * 128], in_=sbank[:, u * 128:(u + 1) * 128], func=AF.Exp,
                                           bias=BIASF[bs][:, j, i:i + 1])
                    return ins
                act(em, rr, ww)
            yield from softmax_attn(b, slot, g, [(lambda j: KT[slot][:, j * 128:(j + 1) * 128], lambda g, q0: QT[slot][:, g * 512 + q0:(g + 1) * 512])],
                         exp_emit, [ob], [db], [t_BIASF[bs]])
            cs = slice(g * 512, (g + 1) * 512)
            act(lambda e: e.activation(out=TMP[0], in_=pb[db][:], func=AF.Ln), [t_pb[db]], [t_TMP[0]])
            act(lambda e: e.activation(out=TMP[0], in_=TMP[0], func=AF.Exp, scale=-1.0), [t_TMP[0]], [t_TMP[0]])
            dve(lambda e: e.tensor_tensor(out=TMP[1], in0=pb[ob][:], in1=TMP[0], op=ALU.mult), [t_pb[ob], t_TMP[0]], [t_TMP[1]])
            dve(lambda e, cs=cs: e.tensor_tensor(out=OT[:, b, cs], in0=TMP[1], in1=OT[:, b, cs], op=ALU.mult), [t_TMP[1], t_OT[b][g]], [t_OT[b][g]])

    def diff_head(b, slot, d):
        for g in range(NG):
            yield from diff_group(b, slot, d, g)

    def diff_group(b, slot, d, g):
        if True:
            def exp_emit(m, j, g, r, q0, sbank, pt, rr, ww):
                def em(e):
                    ins = None
                    for u2 in range(2):
                        lo = max(q0, 256 * u2)
                        hi = 256 * (u2 + 1)
                        if lo >= hi:
                            continue
                        gi = 2 * g + u2
                        ins = e.activation(out=pt[:, lo:hi], in_=sbank[:, lo:hi], func=AF.Exp, bias=biasD[:, d, j, gi:gi + 1])
                    return ins
                act(em, rr, ww)
            maps = [(lambda j, c=c: KT[slot][c * 64:(c + 1) * 64, j * 128:(j + 1) * 128],
                     lambda g, q0, c=c: QT[slot][c * 64:(c + 1) * 64, g * 512 + q0:(g + 1) * 512]) for c in range(2)]
            yield from softmax_attn(b, slot, g, maps, exp_emit, [4, 5], [6, 7], [t_const])
            cs = slice(g * 512, (g + 1) * 512)
            dve(lambda e: e.tensor_copy(out=TMP[1], in_=pb[4][:]), [t_pb[4]], [t_TMP[1]])
            act(lambda e: e.activation(out=TMP[0], in_=pb[6][:], func=AF.Ln), [t_pb[6]], [t_TMP[0]])
            dve(lambda e: e.tensor_copy(out=TMP[2], in_=pb[5][:]), [t_pb[5]], [t_TMP[2]])
            act(lambda e: e.activation(out=TMP[3], in_=pb[7][:], func=AF.Ln), [t_pb[7]], [t_TMP[3]])
            act(lambda e: e.activation(out=TMP[0], in_=TMP[0], func=AF.Exp, scale=-1.0), [t_TMP[0]], [t_TMP[0]])
            act(lambda e: e.activation(out=TMP[3], in_=TMP[3], func=AF.Exp, scale=-1.0), [t_TMP[3]], [t_TMP[3]])
            dve(lambda e: e.tensor_tensor(out=TMP[1], in0=TMP[1], in1=TMP[0], op=ALU.mult), [t_TMP[1], t_TMP[0]], [t_TMP[1]])
            dve(lambda e: e.tensor_tensor(out=TMP[2], in0=TMP[2], in1=TMP[3], op=ALU.mult), [t_TMP[2], t_TMP[3]], [t_TMP[2]])
            dve(lambda e: e.scalar_tensor_tensor(out=TMP[1], in0=TMP[2], scalar=lamt[:, 4:5], in1=TMP[1], op0=ALU.mult, op1=ALU.add),
                [t_TMP[2], t_TMP[1], t_lam], [t_TMP[1]])
            dve(lambda e: e.tensor_tensor(out=TMP[2], in0=TMP[1], in1=TMP[1], op=ALU.mult), [t_TMP[1]], [t_TMP[2]])
            pe(lambda e: e.matmul(pb[2][:], lhsT=meanF[:], rhs=TMP[2], start=True, stop=True), [t_TMP[2], t_const], [t_pb[2]])
            act(lambda e: e.activation(out=TMP[3], in_=pb[2][:], func=AF.Ln, bias=float(SUBLN_EPS)), [t_pb[2]], [t_TMP[3]])
            act(lambda e: e.activation(out=TMP[3], in_=TMP[3], func=AF.Exp, scale=-0.5), [t_TMP[3]], [t_TMP[3]])
            dve(lambda e: e.tensor_tensor(out=TMP[1], in0=TMP[1], in1=TMP[3], op=ALU.mult), [t_TMP[1], t_TMP[3]], [t_TMP[1]])
            dve(lambda e, cs=cs: e.scalar_tensor_tensor(out=OT[:, b, cs], in0=TMP[1], scalar=gcol[:, 0:1], in1=OT[:, b, cs],
                                                       op0=ALU.mult, op1=ALU.mult), [t_TMP[1], t_lam, t_OT[b][g]], [t_OT[b][g]])

    def sb_head(b, slot):
        for g in range(NG):
            yield from sb_group(b, slot, g)

    def sb_group(b, slot, g):
        ob = 4 + g % 2
        if True:
            nblk = 4 * g + 4
            js = list(range(nblk - 1, -1, -1))
            st = {}
            dve(lambda e: e.memset(ACC32, 0.0), (), [t_ACC32])
            pe(lambda e: e.matmul(pb[ob][:], lhsT=zerosB[:], rhs=QT[slot][:, g * 512:(g + 1) * 512], start=True, stop=False),
               [t_const, t_QT[slot][g]], [t_pb[ob]])

            def stage_z(n):
                j = js[n]
                r, q0 = blk_geom(g, j)
                zb = 2 + n % 2
                si = n % 2
                st[n] = (zb, si)
                kt = KT[slot][:, j * 128:(j + 1) * 128]
                qt = QT[slot][:, g * 512 + q0:(g + 1) * 512]
                def mmz(e):
                    ins = e.matmul(pb[zb][:, q0:512], lhsT=kt, rhs=qt, start=True, stop=(r < 0))
                    if r >= 0:
                        ins = e.matmul(pb[zb][:, q0:q0 + 128], lhsT=identB[:], rhs=maskS[:], start=False, stop=True)
                    return ins
                pe(mmz, [t_KT[slot][j // 4], t_QT[slot][g], t_const], [t_pb[zb]])
                act(lambda e: e.activation(out=pb[zb][:, q0:512], in_=pb[zb][:, q0:512], func=AF.Exp), [t_pb[zb]], [t_pb[zb]])
                act(lambda e: e.activation(out=SP_[si][:, q0:512], in_=pb[zb][:, q0:512], func=AF.Ln, bias=1.0), [t_pb[zb]], [t_SP[si]])

            def stage_arg(n):
                j = js[n]
                r, q0 = blk_geom(g, j)
                zb, si = st[n]
                ab = 6 + n % 2
                ptc[0] += 1
                pi = ptc[0] % 3
                st[n] = (zb, si, pi)
                kt = KT[slot][:, j * 128:(j + 1) * 128]
                qt = QT[slot][:, g * 512 + q0:(g + 1) * 512]
                rd = [t_KT[slot][j // 4], t_QT[slot][g], t_SP[si], t_const]
                if n > 0:
                    rd.append(t_ACCB[(n - 1) % 2])

                def mm(e):
                    e.matmul(pb[ab][:, q0:512], lhsT=kt, rhs=qt, start=True, stop=False)
                    if r >= 0:
                        e.matmul(pb[ab][:, q0:q0 + 128], lhsT=identB[:], rhs=maskS[:], start=False, stop=False)
                    ins = e.matmul(pb[ab][:, q0:512], lhsT=negTriB[:], rhs=SP_[si][:, q0:512], start=False, stop=(n == 0))
                    if n > 0:
                        ins = e.matmul(pb[ab][:, q0:512], lhsT=negOnesB[:], rhs=ACCB[(n - 1) % 2][:, q0:512], start=False, stop=True)
                    return ins
                pe(mm, rd, [t_pb[ab]])
                if n < nblk - 1:
                    dve(lambda e: e.tensor_tensor(out=ACC32[:, q0:512], in0=ACC32[:, q0:512], in1=SP_[si][:, q0:512], op=ALU.add),
                        [t_SP[si], t_ACC32], [t_ACC32])
                    dve(lambda e: e.tensor_copy(out=ACCB[n % 2], in_=ACC32), [t_ACC32], [t_ACCB[n % 2]])
                act(lambda e: e.activation(out=PT[pi][:, q0:512], in_=pb[ab][:, q0:512], func=AF.Exp), [t_pb[ab]], [t_PT[pi]])

            def stage_pv(n):
                j = js[n]
                r, q0 = blk_geom(g, j)
                zb, si, pi = st[n]
                last = (n == nblk - 1)
                pe(lambda e: e.matmul(pb[ob][:, q0:512], lhsT=VV[slot][:, j, :], rhs=PT[pi][:, q0:512], start=False, stop=last),
                   [t_V[slot][j // 4], t_PT[pi]], [t_pb[ob]])

            for n in range(nblk + 2):
                if n < nblk:
                    stage_z(n)
                if 0 <= n - 1 < nblk:
                    stage_arg(n - 1)
                if 0 <= n - 2 < nblk:
                    stage_pv(n - 2)
                yield
            cs = slice(g * 512, (g + 1) * 512)
            dve(lambda e, cs=cs: e.tensor_tensor(out=OT[:, b, cs], in0=pb[ob][:], in1=OT[:, b, cs], op=ALU.mult), [t_pb[ob], t_OT[b][g]], [t_OT[b][g]])

    all_qkv = [t for s_ in range(2) for g_ in range(NG) for t in (t_QT[s_][g_], t_KT[s_][g_], t_V[s_][g_])]
    all_misc = t_PT + t_SP + t_ACCB + [t_ACC32] + t_BIASF + t_TMP + t_GT

    def phase3_prologue(l):
        for cb in range(NCB):
            dma("pool", s_wo[cb], wo[cb], wout_d[l, cb], (), [t_wo[cb]] + t_xT)

    def phase3(l, src, dst):
        def emit(e):
            e.dma_start(out=lng_b, in_=lng_d[l:l + 1, :].partition_broadcast(128)).then_inc(s_lnp, 16)
            return e.dma_start(out=lnb_b, in_=lnb_d[l:l + 1, :].partition_broadcast(128)).then_inc(s_lnp, 16)
        sch.add("sp", emit, (), [t_lnp] + all_misc, dma_sem=s_lnp, ndma=2)
        FM = 512
        nch = (D + FM - 1) // FM
        while D % nch:
            nch += 1
        chw = D // nch
        def zload(t):
            zi = t % 3
            dma("sp", s_zin[zi], zsl[zi], src[t * 128:(t + 1) * 128, :], (), [t_z[zi]] + (all_qkv if t < 3 else []))
        for t in range(min(3, NT)):
            zload(t)
        for t in range(NT):
            zi = t % 3
            z = zsl[zi]
            yb = [(t % 2) * NCB + cb for cb in range(NCB)]

            def mm(e, t=t, yb=yb):
                ins = None
                for kc in range(KC):
                    for cb in range(NCB):
                        ins = e.matmul(pb[yb[cb]][:, 0:CBW], lhsT=OT[:, kc, t * 128:(t + 1) * 128], rhs=wo[cb][:, kc, :],
                                       start=(kc == 0), stop=(kc == KC - 1))
                return ins
            pe(mm, t_wo + [t_OT[b][t // 4] for b in range(NH)], [t_pb[i] for i in yb])
            for cb in range(NCB):
                dve(lambda e, cb=cb, z=z, yb=yb: e.scalar_tensor_tensor(out=z[:, cb * CBW:(cb + 1) * CBW], in0=z[:, cb * CBW:(cb + 1) * CBW],
                                                                      scalar=float(DEEPNORM_ALPHA), in1=pb[yb[cb]][:, 0:CBW],
                                                                      op0=ALU.mult, op1=ALU.add), [t_pb[yb[cb]], t_z[zi]], [t_z[zi]])
            for c in range(nch):
                dve(lambda e, c=c, z=z: e.bn_stats(out=bnst[:, c, :], in_=z[:, c * chw:(c + 1) * chw]), [t_z[zi], t_st], [t_st])
            dve(lambda e: e.bn_aggr(out=mv[:, 0:2], in_=bnst[:, 0:nch, :]), [t_st], [t_st])
            act(lambda e: e.activation(out=mv[:, 2:3], in_=mv[:, 1:2], func=AF.Ln, bias=float(LN_EPS)), [t_st], [t_st])
            act(lambda e: e.activation(out=mv[:, 3:4], in_=mv[:, 2:3], func=AF.Exp, scale=-0.5), [t_st], [t_st])
            dve(lambda e: e.scalar_tensor_tensor(out=mv[:, 4:5], in0=mv[:, 0:1], scalar=-1.0, in1=mv[:, 3:4], op0=ALU.mult, op1=ALU.mult),
                [t_st], [t_st])
            act(lambda e, z=z: e.activation(out=z, in_=z, func=AF.Identity, scale=mv[:, 3:4], bias=mv[:, 4:5]), [t_st, t_z[zi]], [t_z[zi], t_st])
            pool(lambda e, z=z: e.tensor_tensor(out=z, in0=z, in1=lng_b, op=ALU.mult), [t_z[zi], t_lnp], [t_z[zi]])
            pool(lambda e, z=z: e.tensor_tensor(out=z, in0=z, in1=lnb_b, op=ALU.add), [t_z[zi], t_lnp], [t_z[zi]])
            dma("sp", s_zout[zi], dst[t * 128:(t + 1) * 128, :], z, [t_z[zi]], ())
            if t + 3 < NT:
                zload(t + 3)

    stop = cfg.get("stop")
    consts()
    sch.barrier()
    for sq in range(NSEQ):
        for l in range(DEPTH):
            if stop == "consts":
                break
            src = x_d[sq] if l == 0 else scr_d[(l - 1) % 2]
            dst = out_d[sq] if l == DEPTH - 1 else scr_d[l % 2]
            phase1(src)
            if cfg.get("p1bar", False):
                sch.barrier()
            if stop == "p1":
                break
            load_params(l)
            if stop == "par":
                break
            def drain(gen):
                for _ in gen:
                    pass

            def interleave(main, side, n_main, n_side):
                n_main = max(1, int(n_main * 0.85))
                i = 0
                emitted = 0
                done = False
                for _ in main:
                    want = min(n_side, (i * n_side) // n_main + 1)
                    while not done and emitted < want:
                        try:
                            next(side)
                            emitted += 1
                        except StopIteration:
                            done = True
                    i += 1
                if not done:
                    drain(side)

            def attn_gen(b, slot):
                if b < NF:
                    yield from fox_head(b, slot, b, b % 2)
                elif b < NF + NS:
                    yield from sb_head(b, slot)
                else:
                    yield from diff_head(b, slot, b - NF - NS)

            def n_attn(b):
                base = sum(4 * g + 5 for g in range(NG))
                if b < NF:
                    return base
                if b < NF + NS:
                    return base + NG
                return 2 * base - NG

            def kind(b):
                return "f" if b < NF else ("s" if b < NF + NS else "d")

            if stop in ("inproj", "fbias"):
                for b in range(NH):
                    drain(inproj_head(l, b, 0, "f"))
                    if stop == "fbias":
                        fox_bias(b, b % 2)
                break
            prev = None
            for b in range(NH):
                ip = inproj_head(l, b, b % 2, kind(b))
                if prev is None:
                    def prep_gen():
                        if NF > 0:
                            yield from fox_prep(l)
                    interleave(ip, prep_gen(), 10 * NG, 4)
                    if NF > 0:
                        fox_bias(0, 0)
                else:
                    if b < NF:
                        fox_bias(b, b % 2)
                    interleave(prev, ip, n_attn(b - 1), 10 * NG)
                prev = attn_gen(b, b % 2)
            if stop == "attn":
                drain(prev)
                break

            def p3pro():
                phase3_prologue(l)
                yield
            interleave(prev, p3pro(), 1, 1)
            phase3(l, src, dst)
            sch.barrier()
    sch.finish()
    sch.emit_all(nc, es)
    es.close()
    return nc


def layout_weights(cfg, w_in, w_out):
    NF, NS, ND = cfg["NF"], cfg["NS"], cfg["ND"]
    NH = NF + NS + ND
    D = 128 * NH
    KC = NH
    DEPTH = w_in.shape[0]
    FW, SW, DW = NF * 128, NS * 128, ND * 128
    offs = []
    for h in range(NF):
        offs.append([k * FW + h * 128 for k in range(4)])
    base = 4 * FW
    for h in range(NS):
        offs.append([base + k * SW + h * 128 for k in range(4)])
    base = 4 * FW + 4 * SW
    for h in range(ND):
        offs.append([base + k * DW + h * 128 for k in range(4)])
    ffo = 4 * FW + 4 * SW + 4 * DW
    w_in_r = np.empty((DEPTH, NH, 4, 128, KC, 128), np.float32)
    for b in range(NH):
        for c in range(4):
            blk = w_in[:, :, offs[b][c]:offs[b][c] + 128]
            w_in_r[:, b, c] = blk.reshape(DEPTH, KC, 128, 128).transpose(0, 2, 1, 3)
    NFp = max(NF, 1)
    w_ff_r = np.zeros((DEPTH, 128, KC, NFp), np.float32)
    if NF > 0:
        w_ff_r[:] = w_in[:, :, ffo:ffo + NF].reshape(DEPTH, KC, 128, NF).transpose(0, 2, 1, 3)
    CBW = min(512, D)
    NCB = D // CBW
    w_out_r = np.ascontiguousarray(
        w_out.reshape(DEPTH, KC, 128, NCB, CBW).transpose(0, 3, 2, 1, 4))
    return w_in_r, w_ff_r, w_out_r


def run(cfg, x, w_in, b_f, diff_lambda, diff_subln_g, w_out, ln_g, ln_b, n_cores):
    NSEQ = cfg["NSEQ"]
    DEPTH = cfg["DEPTH"]
    x = np.ascontiguousarray(np.asarray(x, np.float32))
    w_in_r, w_ff_r, w_out_r = layout_weights(cfg, np.asarray(w_in, np.float32), np.asarray(w_out, np.float32))
    NFp = max(cfg["NF"], 1)
    bfp = np.zeros((DEPTH, NFp), np.float32)
    if cfg["NF"] > 0:
        bfp[:] = np.asarray(b_f, np.float32)
    common = {
        "w_in_r": w_in_r, "w_ff_r": w_ff_r, "w_out_r": w_out_r, "b_f": bfp,
        "diff_lambda": np.ascontiguousarray(np.asarray(diff_lambda, np.float32).reshape(DEPTH, 256)),
        "diff_subln_g": np.ascontiguousarray(np.asarray(diff_subln_g, np.float32)),
        "ln_g": np.ascontiguousarray(np.asarray(ln_g, np.float32)),
        "ln_b": np.ascontiguousarray(np.asarray(ln_b, np.float32)),
    }
    nc = build_nc(cfg)
    in_maps = []
    for c in range(n_cores):
        m = dict(common)
        m["x"] = np.ascontiguousarray(x[c * NSEQ:(c + 1) * NSEQ])
        in_maps.append(m)
    res = run_bass_kernel_spmd(nc, in_maps, core_ids=list(range(n_cores)))
    return np.concatenate([np.asarray(r["out"], np.float32) for r in res.results], axis=0)


def kernel(x, w_in, b_f, diff_lambda, diff_subln_g, w_out, ln_g, ln_b):
    return run(FULL_CFG, x, w_in, b_f, diff_lambda, diff_subln_g, w_out, ln_g, ln_b, 8)
```
